# Optimizing a Trainium2 kernel written in Bass

```python
import math
import jax, jax.numpy as jnp
from jax import lax
import numpy as np

D_MODEL = 2048
BATCH = 4
SEQ = 4096
DEPTH = 1

HEAD_DIM = 128
SB_HEADS = 8
NSA_HEADS = 8
NSA_KV_GROUPS = 2
NSA_HPG = NSA_HEADS // NSA_KV_GROUPS
CMP_BLOCK = 32
CMP_STRIDE = 16
SLC_BLOCK = 64
N_SELECT = 16
WINDOW = 512
Q_BLOCK = 128
SLC_Q_BLOCK = 64
D_FF = 5632
NORM_EPS = 1e-6
NEG_INF = -1e30
FORCED_SCORE = 1e9

IN_SIZES = ([SB_HEADS * HEAD_DIM] * 3 + [NSA_HEADS * HEAD_DIM]
            + [NSA_KV_GROUPS * HEAD_DIM] * 6 + [3 * NSA_HEADS, 2 * D_MODEL])
IN_COLS = sum(IN_SIZES)

kernel_name = "hybrid_stickbreaking_nsa_macaron_block"


def rmsnorm(x, g):
    xf = x.astype(jnp.float32)
    y = xf * lax.rsqrt(jnp.mean(xf * xf, axis=-1, keepdims=True) + NORM_EPS)
    return (y * g.astype(jnp.float32)).astype(x.dtype)


def swiglu_ffn(h, w_in, w_out):
    gate, up = jnp.split(h @ w_in, 2, axis=-1)
    return (jax.nn.silu(gate) * up) @ w_out


def alibi_slopes(n):
    return jnp.asarray(2.0 ** (-8.0 * np.arange(1, n + 1) / n), jnp.float32)


def stick_breaking_attention(q, k, v):
    B, S, H, dh = q.shape
    f32 = jnp.float32
    scale = dh ** -0.5
    qf = jnp.transpose(q, (0, 2, 1, 3)).astype(f32)
    kf = jnp.transpose(k, (0, 2, 1, 3)).astype(f32)
    vf = jnp.transpose(v, (0, 2, 1, 3)).astype(f32)
    key_pos = jnp.arange(S)
    n_blocks = S // Q_BLOCK

    def block(i):
        start = i * Q_BLOCK
        qb = lax.dynamic_slice_in_dim(qf, start, Q_BLOCK, axis=2)
        z = jnp.einsum('bhqd,bhkd->bhqk', qb, kf) * scale
        q_pos = start + jnp.arange(Q_BLOCK)
        past = key_pos[None, :] < q_pos[:, None]
        log_beta = jax.nn.log_sigmoid(z)
        log_1m = jnp.where(past, jax.nn.log_sigmoid(-z), 0.0)
        suffix = lax.cumsum(log_1m, axis=3, reverse=True)
        weight = jnp.where(past, jnp.exp(log_beta + suffix - log_1m), 0.0)
        return jnp.einsum('bhqk,bhkd->bhqd', weight, vf)

    out = lax.map(block, jnp.arange(n_blocks))
    out = jnp.transpose(out, (1, 0, 3, 2, 4)).reshape(B, S, H * dh)
    return out


def compress_blocks(kv, pos, w1, w2):
    B, S, G, dh = kv.shape
    n_cmp = (S - CMP_BLOCK) // CMP_STRIDE + 1
    idx = np.arange(n_cmp)[:, None] * CMP_STRIDE + np.arange(CMP_BLOCK)[None, :]
    blocks = kv[:, idx] + pos[None, None, :, None, :]
    blocks = jnp.transpose(blocks, (0, 1, 3, 2, 4)).reshape(B, n_cmp, G, CMP_BLOCK * dh)
    return jax.nn.gelu(blocks @ w1) @ w2


def nsa_attention(q, k_cmp, v_cmp, k_slc, v_slc, k_win, v_win, gate_logits,
                  cmp_pos_k, cmp_k_w1, cmp_k_w2, cmp_pos_v, cmp_v_w1, cmp_v_w2):
    B, S, G, R, dh = q.shape
    f32 = jnp.float32
    scale = dh ** -0.5
    slopes = alibi_slopes(G * R).reshape(G, R)
    qf = q.astype(f32)
    t = np.arange(S)

    kc = compress_blocks(k_cmp.astype(f32), cmp_pos_k, cmp_k_w1, cmp_k_w2)
    vc = compress_blocks(v_cmp.astype(f32), cmp_pos_v, cmp_v_w1, cmp_v_w2)
    n_cmp = kc.shape[1]
    cmp_end = np.arange(n_cmp) * CMP_STRIDE + CMP_BLOCK - 1
    dist_c_np = t[:, None] - cmp_end[None, :]
    dist_c = jnp.asarray(dist_c_np, f32)
    valid_c = jnp.asarray(dist_c_np >= 0)
    s_c = (jnp.einsum('bqgrd,bkgd->bgrqk', qf, kc) * scale
           - slopes[None, :, :, None, None] * dist_c)
    p_cmp = jax.nn.softmax(jnp.where(valid_c, s_c, NEG_INF), axis=-1) * valid_c
    o_cmp = jnp.einsum('bgrqk,bkgd->bqgrd', p_cmp, vc)

    ratio = SLC_BLOCK // CMP_STRIDE
    span = CMP_BLOCK // CMP_STRIDE
    n_slc = S // SLC_BLOCK
    p_grp = jnp.sum(p_cmp, axis=2)
    need = ratio * (n_slc - 1) + (ratio - 1) + (span - 1) + 1
    p_pad = jnp.pad(p_grp, ((0, 0), (0, 0), (0, 0), (0, max(need - n_cmp, 0))))
    terms = [p_pad[..., m + n: m + n + ratio * (n_slc - 1) + 1: ratio]
             for m in range(ratio) for n in range(span)]
    slc_score = jnp.sum(jnp.stack(terms, axis=0), axis=0)
    blk = np.arange(n_slc)
    cur = t // SLC_BLOCK
    forced = (blk[None, :] == 0) | (blk[None, :] == cur[:, None]) | (blk[None, :] == cur[:, None] - 1)
    valid_blk = blk[None, :] * SLC_BLOCK <= t[:, None]
    slc_score = jnp.where(jnp.asarray(forced), FORCED_SCORE, slc_score)
    slc_score = jnp.where(jnp.asarray(valid_blk), slc_score, NEG_INF)
    n_top = min(N_SELECT, n_slc)
    _, sel_idx = lax.top_k(slc_score, n_top)

    kb = jnp.transpose(k_slc.astype(f32), (0, 2, 1, 3)).reshape(B, G, n_slc, SLC_BLOCK, dh)
    vb = jnp.transpose(v_slc.astype(f32), (0, 2, 1, 3)).reshape(B, G, n_slc, SLC_BLOCK, dh)
    bi = jnp.arange(B)[:, None, None, None]
    gi = jnp.arange(G)[None, :, None, None]
    in_blk = jnp.arange(SLC_BLOCK)

    def slc_block(i):
        start = i * SLC_Q_BLOCK
        qb = lax.dynamic_slice_in_dim(qf, start, SLC_Q_BLOCK, axis=1)
        ib = lax.dynamic_slice_in_dim(sel_idx, start, SLC_Q_BLOCK, axis=2)
        kg = kb[bi, gi, ib].reshape(B, G, SLC_Q_BLOCK, n_top * SLC_BLOCK, dh)
        vg = vb[bi, gi, ib].reshape(B, G, SLC_Q_BLOCK, n_top * SLC_BLOCK, dh)
        pos = (ib[..., None] * SLC_BLOCK + in_blk).reshape(B, G, SLC_Q_BLOCK, n_top * SLC_BLOCK)
        tq = start + jnp.arange(SLC_Q_BLOCK)
        dist = (tq[None, None, :, None] - pos)[:, :, None]
        s = (jnp.einsum('bqgrd,bgqkd->bgrqk', qb, kg) * scale
             - slopes[None, :, :, None, None] * dist.astype(f32))
        p = jax.nn.softmax(jnp.where(dist >= 0, s, NEG_INF), axis=-1)
        return jnp.einsum('bgrqk,bgqkd->bqgrd', p, vg)

    o_slc = lax.map(slc_block, jnp.arange(S // SLC_Q_BLOCK))
    o_slc = jnp.moveaxis(o_slc, 0, 1).reshape(B, S, G, R, dh)

    n_wb = S // Q_BLOCK
    kw_pad = jnp.pad(k_win.astype(f32), ((0, 0), (WINDOW, 0), (0, 0), (0, 0)))
    vw_pad = jnp.pad(v_win.astype(f32), ((0, 0), (WINDOW, 0), (0, 0), (0, 0)))
    widx = np.arange(n_wb)[:, None] * Q_BLOCK + np.arange(WINDOW + Q_BLOCK)[None, :]
    kwb = kw_pad[:, widx]
    vwb = vw_pad[:, widx]
    q_pos = np.arange(n_wb)[:, None] * Q_BLOCK + np.arange(Q_BLOCK)[None, :]
    k_pos = widx - WINDOW
    dist_w_np = q_pos[:, :, None] - k_pos[:, None, :]
    valid_w = jnp.asarray((dist_w_np >= 0) & (dist_w_np < WINDOW) & (k_pos[:, None, :] >= 0))
    dist_w = jnp.asarray(dist_w_np, f32)
    qwb = qf.reshape(B, n_wb, Q_BLOCK, G, R, dh)
    s_w = (jnp.einsum('bnqgrd,bnkgd->bgrnqk', qwb, kwb) * scale
           - slopes[None, :, :, None, None, None] * dist_w)
    p_w = jax.nn.softmax(jnp.where(valid_w, s_w, NEG_INF), axis=-1)
    o_win = jnp.einsum('bgrnqk,bnkgd->bnqgrd', p_w, vwb).reshape(B, S, G, R, dh)

    g = jax.nn.sigmoid(gate_logits.astype(f32)).reshape(B, S, 3, G, R, 1)
    out = g[:, :, 0] * o_cmp + g[:, :, 1] * o_slc + g[:, :, 2] * o_win
    return out.reshape(B, S, G * R * dh)


def hybrid_mixer(h, w_in, cmp_pos_k, cmp_k_w1, cmp_k_w2, cmp_pos_v, cmp_v_w1, cmp_v_w2,
                 w_branch_sb, w_branch_nsa, w_out):
    B, S, _ = h.shape
    proj = h @ w_in
    offsets = np.cumsum(IN_SIZES)[:-1].tolist()
    (q_sb, k_sb, v_sb, q_nsa, k_cmp, v_cmp, k_slc, v_slc, k_win, v_win,
     nsa_gates, merge_logits) = jnp.split(proj, offsets, axis=-1)
    heads = lambda a, n: a.reshape(B, S, n, HEAD_DIM)
    sb_out = stick_breaking_attention(heads(q_sb, SB_HEADS), heads(k_sb, SB_HEADS),
                                      heads(v_sb, SB_HEADS)).astype(h.dtype)
    G = NSA_KV_GROUPS
    nsa_out = nsa_attention(q_nsa.reshape(B, S, G, NSA_HPG, HEAD_DIM),
                            heads(k_cmp, G), heads(v_cmp, G), heads(k_slc, G), heads(v_slc, G),
                            heads(k_win, G), heads(v_win, G), nsa_gates,
                            cmp_pos_k, cmp_k_w1, cmp_k_w2, cmp_pos_v, cmp_v_w1, cmp_v_w2).astype(h.dtype)
    y_sb = sb_out @ w_branch_sb
    y_nsa = nsa_out @ w_branch_nsa
    gates = jax.nn.sigmoid(merge_logits.astype(jnp.float32)).reshape(B, S, 2, D_MODEL).astype(h.dtype)
    merged = gates[:, :, 0] * y_sb + gates[:, :, 1] * y_nsa
    return merged @ w_out


def setup_inputs(seed: int = 0) -> dict:
    key = jax.random.key(seed)
    ks = jax.random.split(key, 22)
    f32 = jnp.float32
    L = DEPTH
    dense = lambda k, shape, fan_in: jax.random.normal(k, shape, f32) * fan_in ** -0.5
    gain = lambda k: 1.0 + 0.02 * jax.random.normal(k, (L, D_MODEL), f32)
    return {
        "x": jax.random.normal(ks[0], (BATCH, SEQ, D_MODEL), f32),
        "ffn1_pre_g": gain(ks[1]),
        "ffn1_w_in": dense(ks[2], (L, D_MODEL, 2 * D_FF), D_MODEL),
        "ffn1_w_out": dense(ks[3], (L, D_FF, D_MODEL), D_FF),
        "ffn1_post_g": gain(ks[4]),
        "mix_pre_g": gain(ks[5]),
        "w_in": dense(ks[6], (L, D_MODEL, IN_COLS), D_MODEL),
        "cmp_pos_k": 0.02 * jax.random.normal(ks[7], (L, CMP_BLOCK, HEAD_DIM), f32),
        "cmp_k_w1": dense(ks[8], (L, CMP_BLOCK * HEAD_DIM, HEAD_DIM), CMP_BLOCK * HEAD_DIM),
        "cmp_k_w2": dense(ks[9], (L, HEAD_DIM, HEAD_DIM), HEAD_DIM),
        "cmp_pos_v": 0.02 * jax.random.normal(ks[10], (L, CMP_BLOCK, HEAD_DIM), f32),
        "cmp_v_w1": dense(ks[11], (L, CMP_BLOCK * HEAD_DIM, HEAD_DIM), CMP_BLOCK * HEAD_DIM),
        "cmp_v_w2": dense(ks[12], (L, HEAD_DIM, HEAD_DIM), HEAD_DIM),
        "w_branch_sb": dense(ks[13], (L, SB_HEADS * HEAD_DIM, D_MODEL), SB_HEADS * HEAD_DIM),
        "w_branch_nsa": dense(ks[14], (L, NSA_HEADS * HEAD_DIM, D_MODEL), NSA_HEADS * HEAD_DIM),
        "w_out": dense(ks[15], (L, D_MODEL, D_MODEL), D_MODEL),
        "mix_post_g": gain(ks[16]),
        "ffn2_pre_g": gain(ks[17]),
        "ffn2_w_in": dense(ks[18], (L, D_MODEL, 2 * D_FF), D_MODEL),
        "ffn2_w_out": dense(ks[19], (L, D_FF, D_MODEL), D_FF),
        "ffn2_post_g": gain(ks[20]),
    }


def reference(x, ffn1_pre_g, ffn1_w_in, ffn1_w_out, ffn1_post_g, mix_pre_g, w_in,
              cmp_pos_k, cmp_k_w1, cmp_k_w2, cmp_pos_v, cmp_v_w1, cmp_v_w2,
              w_branch_sb, w_branch_nsa, w_out, mix_post_g,
              ffn2_pre_g, ffn2_w_in, ffn2_w_out, ffn2_post_g):
    for l in range(DEPTH):
        h = rmsnorm(x, ffn1_pre_g[l])
        x = x + 0.5 * rmsnorm(swiglu_ffn(h, ffn1_w_in[l], ffn1_w_out[l]), ffn1_post_g[l])
        h = rmsnorm(x, mix_pre_g[l])
        y = hybrid_mixer(h, w_in[l], cmp_pos_k[l], cmp_k_w1[l], cmp_k_w2[l],
                         cmp_pos_v[l], cmp_v_w1[l], cmp_v_w2[l],
                         w_branch_sb[l], w_branch_nsa[l], w_out[l])
        x = x + rmsnorm(y, mix_post_g[l])
        h = rmsnorm(x, ffn2_pre_g[l])
        x = x + 0.5 * rmsnorm(swiglu_ffn(h, ffn2_w_in[l], ffn2_w_out[l]), ffn2_post_g[l])
    return x
```

```python
import contextlib
import numpy as np
import ml_dtypes
import concourse.bass as bass
import concourse.mybir as mybir
from concourse.bass_utils import run_bass_kernel_spmd

F32 = mybir.dt.float32
BF16 = mybir.dt.bfloat16
AF = mybir.ActivationFunctionType
ALU = mybir.AluOpType

D = 2048
DFF = 5632
NCH = 16
NFF = 44
HD = 128
NEG = -30000.0
NO_SEL = False
ENGS = ["pe", "act", "dve", "pool", "sp"]
N_DMA_SEMS = 8
SLOPES = [2.0 ** (-(h + 1)) for h in range(8)]
C_QSB, C_KSB, C_VSB, C_QNSA = 0, 1024, 2048, 3072
C_KCMP, C_VCMP, C_KSLC, C_VSLC, C_KWIN, C_VWIN = 4096, 4352, 4608, 4864, 5120, 5376
C_GATE, C_MERGE = 5632, 5656
INCOLS = 9752


class Tok:
    __slots__ = ("sk", "idx", "needed", "val")

    def __init__(self, sk, idx):
        self.sk, self.idx, self.needed, self.val = sk, idx, False, None


class _Rec:
    def __init__(self):
        self.call = None

    def __getattr__(self, name):
        def f(*args, **kwargs):
            self.call = (name, args, kwargs)
            return None
        return f


def _freeze(fn):
    rec = _Rec()
    fn(rec)
    name, args, kwargs = rec.call
    return lambda engine: getattr(engine, name)(*args, **kwargs)


class Prog:
    def __init__(self, nc):
        self.nc = nc
        self.q = {e: [] for e in ENGS}
        self.cnt = {}
        self.last_w = {}
        self.readers = {}
        self.waited = {e: {} for e in ENGS}
        self.dma_rr = {e: 0 for e in ENGS}
        self.dma_last = {}
        self.all_tokens = {}
        self.pending = {e: [] for e in ENGS}

    def _newtok(self, sk):
        i = self.cnt.get(sk, 0) + 1
        self.cnt[sk] = i
        t = Tok(sk, i)
        self.all_tokens.setdefault(sk, []).append(t)
        return t

    def write_deps(self, keys):
        deps = []
        for w in keys:
            t = self.last_w.get(w)
            if t is not None:
                deps.append(t)
            deps.extend(self.readers.get(w, {}).values())
        return deps

    @staticmethod
    def _expand(keys):
        out = []
        for k in keys:
            if k == "ps7":
                out += ["ps7a", "ps7b"]
            else:
                out.append(k)
        return out

    def issue(self, eng, fn, reads=(), writes=(), dma=False, pre_deps=None, dgroup=None):
        fn = _freeze(fn)
        reads = self._expand(reads)
        writes = self._expand(writes)
        deps = list(self.pending[eng])
        self.pending[eng] = []
        for r in reads:
            t = self.last_w.get(r)
            if t is not None:
                deps.append(t)
        if pre_deps is not None:
            deps.extend(pre_deps)
        else:
            deps.extend(self.write_deps(writes))
        if dma:
            grp = dgroup or eng
            j = self.dma_rr.get(grp, 0)
            self.dma_rr[grp] = (j + 1) % N_DMA_SEMS
            sk = "D_%s_%d" % (grp, j)
            prev = self.dma_last.get(sk)
            if prev is not None:
                deps.append(prev)
            tok = self._newtok(sk)
            self.dma_last[sk] = tok
        else:
            tok = self._newtok("E_" + eng)
        waits = []
        wd = self.waited[eng]
        for t in deps:
            if eng == "pe" and t.sk == "E_pe":
                continue
            if wd.get(t.sk, 0) >= t.idx:
                continue
            wd[t.sk] = t.idx
            t.needed = True
            waits.append(t)
        self.q[eng].append((waits, fn, tok, dma))
        for r in reads:
            self.readers.setdefault(r, {})[tok.sk] = tok
        for w in writes:
            self.last_w[w] = tok
            self.readers[w] = {}
        return tok

    def fence(self, engs=("pe", "act", "dve", "pool")):
        toks = [ts[-1] for sk, ts in self.all_tokens.items() if not (sk.startswith("D_sp") or sk.startswith("D_pk"))]
        for e in engs:
            self.pending[e] = list(toks)

    def finish(self, eng="sp"):
        waits = []
        for ts in self.all_tokens.values():
            ts[-1].needed = True
            waits.append(ts[-1])
        self.q[eng].append((waits, None, None, False))

    def emit(self):
        nc = self.nc
        for sk, toks in self.all_tokens.items():
            v = 0
            step = 16 if sk.startswith("D_") else 1
            for t in toks:
                if t.needed:
                    v += step
                    t.val = v
        with contextlib.ExitStack() as es:
            es.enter_context(nc.allow_low_precision("bf16 matmul operands by design; fp32 PSUM accumulation"))
            sems = {sk: es.enter_context(nc.semaphore(sk)) for sk in self.all_tokens}
            block = es.enter_context(nc.Block())
            engmap = {"pe": block.tensor, "act": block.scalar, "dve": block.vector,
                      "pool": block.gpsimd, "sp": block.sync}

            def mk(e):
                lst = self.q[e]

                def body(engine):
                    for waits, fn, tok, dma in lst:
                        for t in waits:
                            engine.wait_ge(sems[t.sk], t.val)
                        if fn is None:
                            continue
                        ins = fn(engine)
                        if tok.needed:
                            ins.then_inc(sems[tok.sk], 16 if dma else 1)
                return body

            for e in ENGS:
                if self.q[e]:
                    engmap[e](mk(e))


def bf(a):
    return np.asarray(a, np.float32).astype(ml_dtypes.bfloat16)


def make_tables(S, c):
    NSLOT = S // 1024
    NKB = S // 128
    T = {}
    sel = np.zeros((128, 2 * NSLOT), np.float32)
    for i in range(NSLOT):
        sel[:, 2 * i + c] = 1.0
    T["t_sel"] = sel
    p = np.arange(128)[:, None]
    t = np.arange(512)[None, :]
    sbm = np.zeros((NSLOT, 128, 8, 512), np.float32)
    slm = np.zeros((NSLOT, 128, 8, 512), np.float32)
    wnm = np.zeros((NSLOT, 128, 12, 512), np.float32)
    cpm = np.zeros((NSLOT, 64, NSLOT, 512), np.float32)
    csh = np.zeros((NSLOT, 8, 512), np.float32)
    frc = np.zeros((NSLOT, 128, 4, 64), np.float32)
    n_cmp = (S - 32) // 16 + 1
    for i in range(NSLOT):
        q = 1024 * i + 512 * c + t
        for m in range(8):
            k = 1024 * i + 128 * m + p
            sbm[i, :, m, :] = np.where(k < q, 0.0, NEG)
            slm[i, :, m, :] = np.where(k <= q, 0.0, NEG)
        for m in range(12):
            k = 1024 * i - 512 + 128 * m + p
            ok = (q - k >= 0) & (q - k < 512) & (k >= 0)
            wnm[i, :, m, :] = np.where(ok, 0.0, NEG)
        for g in range(NSLOT):
            n = 64 * g - 1 + np.arange(64)[:, None]
            ok = (n >= 0) & (n < n_cmp) & (16 * n + 31 <= q)
            cpm[i, :, g, :] = np.where(ok, 0.0, NEG)
        for h in range(8):
            csh[i, h, :] = -SLOPES[h] * (512 * c + np.arange(512))
        for ts in range(4):
            tq = 1024 * i + 512 * c + 128 * ts + np.arange(128)[:, None]
            blk = np.arange(64)[None, :]
            cur = tq // 64
            valid = blk * 64 <= tq
            f = np.where(blk == 0, 1e9, 0.0)
            f = np.where(blk == cur - 1, 2e9, f)
            f = np.where(blk == cur, 3e9, f)
            f = np.where(valid, f, -1e30)
            frc[i, :, ts, :] = f
    T["t_sbm"], T["t_slm"], T["t_wnm"], T["t_cpm"] = bf(sbm), bf(slm), bf(wnm), bf(cpm)
    T["t_csh"] = bf(csh)
    T["t_frc"] = frc
    OFFS = max(8 * (NSLOT - 1), 4)
    NREL = OFFS + 8
    bc = np.zeros((128, 8, NREL), np.float32)
    for h in range(8):
        for r in range(NREL):
            bc[:, h, r] = SLOPES[h] * (128 * (r - OFFS) + np.arange(128))
    T["t_bc"] = bc
    bcc = np.zeros((64, 8, NSLOT), np.float32)
    for h in range(8):
        for dlt in range(NSLOT):
            bcc[:, h, dlt] = SLOPES[h] * (16 * np.arange(64) + 15 - 1024 * dlt)
    T["t_bcc"] = bcc
    selh = np.zeros((8, 8, 128), np.float32)
    for h in range(8):
        selh[h, h, :] = 1.0
    T["t_selh"] = bf(selh)
    selg = np.zeros((24, 24, 128), np.float32)
    for k in range(24):
        selg[k, k, :] = 1.0
    T["t_selg"] = bf(selg)
    ne = np.zeros((65, NKB, 128), np.float32)
    ne[64] = 1.0
    for kb in range(NKB):
        for s in range(128):
            ne[(kb * 128 + s) // 64, kb, s] = NEG
    T["t_negexp"] = bf(ne)
    ms = np.zeros((64, NSLOT, 64), np.float32)
    wts = {0: 1.0, 1: 2.0, 2: 2.0, 3: 2.0, 4: 1.0}
    for g in range(NSLOT):
        for pp in range(64):
            n = 64 * g - 1 + pp
            if n < 0 or n >= n_cmp:
                continue
            for j in range(64):
                o = n - 4 * j
                if o in wts:
                    ms[pp, g, j] = wts[o]
    T["t_mslc"] = bf(ms)
    T["t_ident"] = bf(np.eye(128))
    T["t_ones"] = bf(np.ones((128, 128)))
    T["t_negones"] = bf(-np.ones((128, 128)))
    T["t_onesf"] = np.ones((128, 128), np.float32)
    jj = np.arange(128)[:, None]
    ss = np.arange(128)[None, :]
    T["t_negtri"] = bf(np.where(jj >= ss, -1.0, 0.0))
    return T


TABLE_DT = {"t_sel": F32, "t_frc": F32, "t_bc": F32, "t_bcc": F32, "t_onesf": F32}


def build_program(S, dbg=False):
    NSLOT = S // 1024
    NKB = S // 128
    OFFS = max(8 * (NSLOT - 1), 4)
    NREL = OFFS + 8
    nc = bass.Bass("TRN2", target_bir_lowering=False)

    def din(name, shape, dt=F32):
        return nc.dram_tensor(name, list(shape), dt, kind="ExternalInput").ap()

    x = din("x", [S, D])
    gains = din("gains", [6, D])
    w1i, w1o = din("ffn1_w_in", [D, 2 * DFF]), din("ffn1_w_out", [DFF, D])
    w2i, w2o = din("ffn2_w_in", [D, 2 * DFF]), din("ffn2_w_out", [DFF, D])
    win = din("w_in", [D, INCOLS])
    cw1 = [din("cmp_k_w1", [4096, 128]), din("cmp_v_w1", [4096, 128])]
    cw2 = [din("cmp_k_w2", [128, 128]), din("cmp_v_w2", [128, 128])]
    cpos = [din("cmp_pos_k", [32, 128]), din("cmp_pos_v", [32, 128])]
    wbsb, wbnsa, wout = din("w_branch_sb", [1024, D]), din("w_branch_nsa", [1024, D]), din("w_out", [D, D])
    tsh = make_tables(S, 0)
    tin = {k: din(k, v.shape, TABLE_DT.get(k, BF16)) for k, v in tsh.items()}
    out = nc.dram_tensor("out", [NSLOT * 512, D], F32, kind="ExternalOutput").ap()
    dbg_out = {}
    if dbg:
        dbg_out["d_x1"] = nc.dram_tensor("d_x1", [NSLOT * 512, D], F32, kind="ExternalOutput").ap()
        dbg_out["d_sb"] = nc.dram_tensor("d_sb", [NSLOT, 8, 128, 512], BF16, kind="ExternalOutput").ap()
        dbg_out["d_nsa"] = nc.dram_tensor("d_nsa", [NSLOT, 8, 128, 512], BF16, kind="ExternalOutput").ap()
        dbg_out["d_x2"] = nc.dram_tensor("d_x2", [NSLOT * 512, D], F32, kind="ExternalOutput").ap()
        dbg_out["d_hT"] = nc.dram_tensor("d_hT", [128, 16, 512], BF16, kind="ExternalOutput").ap()
        dbg_out["d_aT"] = nc.dram_tensor("d_aT", [128, NFF, 512], BF16, kind="ExternalOutput").ap()
        dbg_out["d_xs"] = nc.dram_tensor("d_xs", [4, 128, D], F32, kind="ExternalOutput").ap()
        dbg_out["d_y"] = nc.dram_tensor("d_y", [4, 128, D], F32, kind="ExternalOutput").ap()
        dbg_out["d_hbf"] = nc.dram_tensor("d_hbf", [128, 4, D], BF16, kind="ExternalOutput").ap()
        dbg_out["d_id"] = nc.dram_tensor("d_id", [128, 128], BF16, kind="ExternalOutput").ap()
        dbg_out["d_gcol"] = nc.dram_tensor("d_gcol", [128, 48], F32, kind="ExternalOutput").ap()
        dbg_out["d_stat"] = nc.dram_tensor("d_stat", [4, 128, 16], F32, kind="ExternalOutput").ap()

    def dscr(name, shape):
        return nc.dram_tensor(name, list(shape), BF16, kind="Internal").ap()

    KT_SB = dscr("kt_sb", [8, 128, S])
    V_SB = dscr("v_sb", [8, 128, NKB, 128])
    KT_CMP = dscr("kt_cmp", [2, 2, 128, S])
    KT_SLC, KT_WIN = dscr("kt_slc", [2, 128, S]), dscr("kt_win", [2, 128, S])
    V_SLC, V_WIN = dscr("v_slc", [2, 128, NKB, 128]), dscr("v_win", [2, 128, NKB, 128])

    P = Prog(nc)
    I = P.issue

    def sb(name, cols, dt):
        return nc.alloc_sbuf_tensor(name, [128, cols], dt)

    X1OWN = sb("x1own", 4 * D, F32)
    WST = [sb("wst%d" % i, 4096, BF16) for i in range(3)]
    KC = sb("kc", 2 * NSLOT * 64, BF16)
    VC = sb("vc", 2 * NSLOT * 128, BF16)
    CW2 = sb("cw2", 256, BF16)
    POSB = sb("posb", 2, F32)
    GCOL = sb("gcol", 3 * 16, F32)
    STAT = sb("stat", 32, F32)
    tb = {}
    for k, v in tsh.items():
        cols = int(np.prod(v.shape[1:]))
        tb[k] = None
    REGION_COLS = 62464
    REGION = sb("region", REGION_COLS, BF16)
    PSALL = nc.alloc_psum_tensor("psall", [128, 4096], F32)
    PSB = [PSALL[:, b * 512:(b + 1) * 512] for b in range(8)]
    PSBF = PSALL[:, 7 * 512:8 * 512].bitcast(BF16)
    psk = ["ps%d" % b for b in range(8)]

    static_tabs = ["t_sel", "t_bc", "t_bcc", "t_selh", "t_selg", "t_negexp", "t_mslc", "t_ident", "t_ones",
                   "t_negones", "t_negtri", "t_onesf"]
    for k in static_tabs:
        v = tsh[k]
        cols = int(np.prod(v.shape[1:]))
        dt = TABLE_DT.get(k, BF16)
        t_ = nc.alloc_sbuf_tensor("s_" + k, [128, cols], dt)
        tb[k] = t_
        npart = v.shape[0]
        src = tin[k]
        if len(v.shape) == 3:
            src = src.rearrange("p a b -> p (a b)")
        I("pool", lambda e, t_=t_, npart=npart, src=src: e.dma_start(out=t_[0:npart, :], in_=src), writes=[k], dma=True)
    IDENT = tb["t_ident"]
    ONESB = tb["t_ones"]
    NEGONES = tb["t_negones"]
    NEGTRI = tb["t_negtri"]
    SELT = tb["t_sel"]
    BC = tb["t_bc"]
    BCC = tb["t_bcc"]
    SELH = tb["t_selh"]
    SELG = tb["t_selg"]
    NEGEXP = tb["t_negexp"]
    MSLC = tb["t_mslc"]
    ONESF = tb["t_onesf"]

    for gi, row in enumerate([0, 2, 4]):
        I("pool", lambda e, gi=gi, row=row: e.dma_start(out=GCOL[:, gi * 16:(gi + 1) * 16],
                                                     in_=gains[row, :].rearrange("(c p) -> p c", p=128),
                                                     allow_slow_non_contiguous=True),
          writes=["gcol"], dma=True)
    I("pool", lambda e: e.dma_start(out=CW2[:, 0:128], in_=cw2[0]), writes=["cw2"], dma=True)
    I("pool", lambda e: e.dma_start(out=CW2[:, 128:256], in_=cw2[1]), writes=["cw2"], dma=True)

    roff = [0]

    def rreset(o=0):
        roff[0] = o

    def carve(cols, dt):
        n = cols * 2 if dt == F32 else cols
        a = REGION[:, roff[0]:roff[0] + n]
        roff[0] += n
        assert roff[0] <= REGION_COLS, roff[0]
        return a.bitcast(F32) if dt == F32 else a

    wrr = [0]
    NPACK = 256

    packed = {}
    pack_specs = []
    PACK = nc.dram_tensor("wpack", [NPACK, 128, 4096], BF16, kind="Internal").ap()

    def register(jid, parts):
        assert jid not in packed, jid
        packed[jid] = len(pack_specs)
        assert len(pack_specs) < NPACK
        pack_specs.append([jid, parts, False])

    def part_views(base, parts):
        outs = []
        o = 0
        for (src_ap, shape) in parts:
            n = int(np.prod(shape[1:]))
            v = base[:, o:o + n]
            if len(shape) == 3:
                v = v.rearrange("p (a b) -> p a b", a=shape[1])
            outs.append(v)
            o += n
        return outs, o

    def emit_pack(idx):
        jid, parts, emitted = pack_specs[idx]
        if emitted:
            return
        pack_specs[idx][2] = True
        views, _ = part_views(PACK[idx], parts)
        for pi, ((src_ap, shape), v) in enumerate(zip(parts, views)):
            I("pool", lambda e: e.dma_start(out=v, in_=src_ap), writes=["pack%d_%d" % (idx, pi)], dma=True, dgroup="pk")

    pack_next = [0]

    def pack_more(n):
        while n > 0 and pack_next[0] < len(pack_specs):
            if not pack_specs[pack_next[0]][2]:
                emit_pack(pack_next[0])
                n -= 1
            pack_next[0] += 1

    def stage2(srcs, shapes, jid):
        i = wrr[0]
        wrr[0] = (i + 1) % 3
        keys = ["wst%da" % i, "wst%db" % i]
        idx = packed[jid]
        emit_pack(idx)
        parts = pack_specs[idx][1]
        views, tot = part_views(WST[i], parts)
        I("sp", lambda e: e.dma_start(out=WST[i][:, 0:tot], in_=PACK[idx][:, 0:tot]),
          reads=["pack%d_%d" % (idx, pi) for pi in range(len(parts))], writes=keys, dma=True)
        return [(v, keys) for v in views]

    def stage(src_ap, shape, jid):
        (v, keys), = stage2([src_ap], [shape], jid)
        return v, keys

    def pipeline(jobs, depth=2):
        handles = {}
        n = len(jobs)
        for j in range(min(depth, n)):
            handles[j] = jobs[j][0]()
        for j in range(n):
            jobs[j][1](handles.pop(j))
            if j + depth < n:
                handles[j + depth] = jobs[j + depth][0]()

    def wview(w, p=128):
        return w.rearrange("(k p) c -> p k c", p=p)

    def register_all():
        def reg_ffn(w_in_d, w_out_d):
            wvi = wview(w_in_d)
            for c in range(NFF):
                register((w_in_d.tensor.name, "in", c), [(wvi[:, :, c * 128:(c + 1) * 128], [128, 16, 128]),
                                                        (wvi[:, :, DFF + c * 128:DFF + (c + 1) * 128], [128, 16, 128])])
            wvo = wview(w_out_d)
            for k0 in range(0, NFF, 2):
                register((w_out_d.tensor.name, "out", k0), [(wvo[:, k0:k0 + 2, :], [128, 2, 2048])])
        reg_ffn(w1i, w1o)
        wv = wview(win)
        kvcols = [C_KSB + 128 * h for h in range(8)] + [C_KCMP + 128 * g for g in range(2)] + [C_VCMP + 128 * g for g in range(2)] \
            + [C_KSLC + 128 * g for g in range(2)] + [C_KWIN + 128 * g for g in range(2)]
        for col0 in kvcols:
            register(("win", col0), [(wv[:, :, col0:col0 + 128], [128, 16, 128])])
        for c0 in (C_VSB, C_VSB + 512):
            for kh in range(2):
                register(("winv", c0, kh), [(wv[:, kh * 8:(kh + 1) * 8, c0:c0 + 512], [128, 8, 512])])
        for kh in range(2):
            register(("winv2", kh), [(wv[:, kh * 8:(kh + 1) * 8, C_VSLC:C_VSLC + 256], [128, 8, 256]),
                                     (wv[:, kh * 8:(kh + 1) * 8, C_VWIN:C_VWIN + 256], [128, 8, 256])])
        for h in range(16):
            col0 = (C_QSB + 128 * h) if h < 8 else (C_QNSA + 128 * (h - 8))
            register(("win", col0), [(wv[:, :, col0:col0 + 128], [128, 16, 128])])
        register(("win", C_GATE), [(wv[:, :, C_GATE:C_GATE + 24], [128, 16, 24])])
        for kv in range(2):
            register(("cw1", kv), [(cw1[kv].rearrange("(l d) o -> d l o", d=128), [128, 32, 128])])
        for c in range(16):
            register(("wb", c), [(wbsb.rearrange("(h p) c -> p h c", p=128)[:, :, c * 128:(c + 1) * 128], [128, 8, 128]),
                                 (wbnsa.rearrange("(h p) c -> p h c", p=128)[:, :, c * 128:(c + 1) * 128], [128, 8, 128])])
            register(("wm", c), [(wv[:, :, C_MERGE + c * 128:C_MERGE + (c + 1) * 128], [128, 16, 128]),
                                 (wv[:, :, C_MERGE + 2048 + c * 128:C_MERGE + 2048 + (c + 1) * 128], [128, 16, 128])])
        wvo = wview(wout)
        for k0 in range(0, 16, 2):
            register((wout.tensor.name, "out", k0), [(wvo[:, k0:k0 + 2, :], [128, 2, 2048])])
        reg_ffn(w2i, w2o)

    register_all()

    stat_i = [0]

    def stats_begin(src_ap, src_keys, junk_ap, junk_key):
        j = stat_i[0] % 8
        stat_i[0] += 1
        ss = STAT[:, 4 * j:4 * j + 1]
        rs = STAT[:, 4 * j + 1:4 * j + 2]
        k = "stat%d" % j
        I("dve", lambda e: e.memset(ss, 0.0), writes=[k])
        I("act", lambda e: e.activation(out=junk_ap, in_=src_ap, func=AF.Square, accum_out=ss),
          reads=list(src_keys) + [k], writes=[junk_key, k])
        return ss, rs, k

    def stats_finish(ss, rs, k, coef=1.0):
        I("dve", lambda e: e.tensor_scalar(out=rs, in0=ss, scalar1=1.0 / (D * coef * coef), scalar2=1e-6 / (coef * coef), op0=ALU.mult, op1=ALU.add),
          reads=[k], writes=[k])
        I("dve", lambda e: e.reciprocal(out=rs, in_=rs), reads=[k], writes=[k])
        I("act", lambda e: e.activation(out=rs, in_=rs, func=AF.Sqrt), reads=[k], writes=[k])

    def rms_stats(src_ap, src_keys, junk_ap, junk_key):
        ss, rs, k = stats_begin(src_ap, src_keys, junk_ap, junk_key)
        stats_finish(ss, rs, k)
        return rs, k

    def transposes_to_hT(hbf_ap, hbf_key, hT, hT_key, ts, gi):
        for half in range(2):
            for c8 in range(8):
                cc = half * 8 + c8
                I("pe", lambda e: e.transpose(out=PSBF[:, c8 * 128:(c8 + 1) * 128], in_=hbf_ap[:, cc * 128:(cc + 1) * 128], identity=IDENT[:, :]),
                  reads=[hbf_key, "t_ident"], writes=[psk[7]])
            gsl = GCOL[:, gi * 16 + half * 8: gi * 16 + half * 8 + 8].unsqueeze(2).to_broadcast([128, 8, 128])
            I("dve", lambda e: e.tensor_tensor(out=hT[:, half * 8:(half + 1) * 8, ts * 128:(ts + 1) * 128],
                                               in0=PSBF.rearrange("p (k t) -> p k t", k=8), in1=gsl, op=ALU.mult),
              reads=[psk[7], "gcol"], writes=[hT_key])

    def prenorm(src_ap, src_keys, junk_ap, junk_key, hbf_ap, hbf_key):
        rs, k = rms_stats(src_ap, src_keys, junk_ap, junk_key)
        I("dve", lambda e: e.tensor_scalar(out=hbf_ap, in0=src_ap, scalar1=rs, scalar2=1.0, op0=ALU.mult, op1=ALU.mult),
          reads=list(src_keys) + [k], writes=[hbf_key])

    def ffn_in(w_in_d, hT, hT_key, aT, sg):
        wv = wview(w_in_d)
        jobs = []
        for c in range(NFF):
            def load(c=c):
                return stage2([wv[:, :, c * 128:(c + 1) * 128], wv[:, :, DFF + c * 128:DFF + (c + 1) * 128]],
                              [[128, 16, 128], [128, 16, 128]], (w_in_d.tensor.name, "in", c))

            def comp(h, c=c):
                pack_more(2)
                b0 = (c % 2) * 2
                for j, (wt, wkeys) in enumerate(h):
                    for k in range(16):
                        I("pe", lambda e, wt=wt, k=k, b=b0 + j: e.matmul(PSB[b], lhsT=wt[:, k, :], rhs=hT[:, k, :],
                                                                         start=(k == 0), stop=(k == 15)),
                          reads=wkeys + [hT_key], writes=[psk[b0 + j]])
                s_ = sg[c % 2]
                sk_ = "sg%d" % (c % 2)
                I("act", lambda e, s_=s_, b=b0: e.activation(out=s_, in_=PSB[b], func=AF.Silu), reads=[psk[b0]], writes=[sk_])
                I("dve", lambda e, s_=s_, b=b0 + 1, c=c: e.tensor_tensor(out=aT[:, c, :], in0=s_, in1=PSB[b], op=ALU.mult),
                  reads=[sk_, psk[b0 + 1]], writes=["aT%d" % c])
            jobs.append((load, comp))
        pipeline(jobs, 2)

    def tok_out(lhs_fn, lhs_keys_fn, nk, w_d, epilogue, GB, TMPS, coef, pre_pair=None):
        wv = wview(w_d)
        for pair in range(2):
            if pre_pair is not None:
                pre_pair(pair)
            jobs = []
            for k0 in range(0, nk, 2):
                def load(k0=k0):
                    return stage(wv[:, k0:k0 + 2, :], [128, 2, 2048], (w_d.tensor.name, "out", k0))

                def comp(h, k0=k0, pair=pair):
                    pack_more(2)
                    wt, key = h
                    for kk in range(2):
                        k = k0 + kk
                        for tl in range(2):
                            ts = pair * 2 + tl
                            for cp in range(4):
                                b = tl * 4 + cp
                                I("pe", lambda e, k=k, kk=kk, ts=ts, cp=cp, b=b: e.matmul(
                                    PSB[b], lhsT=lhs_fn(k)[:, ts * 128:(ts + 1) * 128],
                                    rhs=wt[:, kk, cp * 512:(cp + 1) * 512], start=(k == 0), stop=(k == nk - 1)),
                                  reads=key + lhs_keys_fn(k), writes=[psk[b]])
                jobs.append((load, comp))
            pipeline(jobs, 2)
            st = []
            for tl in range(2):
                ps_ap = PSALL[:, tl * 2048:(tl + 1) * 2048]
                ps_keys = [psk[tl * 4 + cp] for cp in range(4)]
                tm, tk = TMPS[tl], "tmp%d" % tl
                ss, rs, k = stats_begin(ps_ap, ps_keys, tm, tk)
                I("dve", lambda e: e.tensor_tensor(out=tm, in0=ps_ap, in1=GB, op=ALU.mult), reads=ps_keys + ["GB"], writes=[tk])
                st.append((ss, rs, k, tm, tk))
            for tl in range(2):
                ss, rs, k, tm, tk = st[tl]
                stats_finish(ss, rs, k, coef)
                epilogue(pair * 2 + tl, rs, k, tm, tk)

    def resid_update(xres_ap, xres_keys, rs, k, tm, tk):
        I("dve", lambda e: e.scalar_tensor_tensor(out=xres_ap, in0=tm, scalar=rs, in1=xres_ap, op0=ALU.mult, op1=ALU.add),
          reads=[tk, k] + list(xres_keys), writes=list(xres_keys))

    def load_GB(GB, row):
        I("pool", lambda e: e.dma_start(out=GB, in_=gains[row, :].partition_broadcast(128)), writes=["GB"], dma=True)

    def fm_proj_job(col0, ncols, hT, hT_key, bank, consume):
        wv = wview(win)

        def load():
            return stage(wv[:, :, col0:col0 + ncols], [128, 16, ncols], ("win", col0))

        def comp(h):
            pack_more(1)
            wt, key = h
            for k in range(16):
                I("pe", lambda e, k=k: e.matmul(PSB[bank][0:ncols, :], lhsT=wt[:, k, :], rhs=hT[:, k, :],
                                                start=(k == 0), stop=(k == 15)),
                  reads=key + [hT_key], writes=[psk[bank]])
            consume(PSB[bank][0:ncols, :], psk[bank])
        return (load, comp)

    for i in range(NSLOT):
        Nk = 1024 * (i + 1)
        nkb = 8 * (i + 1)
        for j in range(2):
            if j == 0:
                P.fence()
            T0 = 1024 * i + 512 * j
            rreset()
            hT = carve(16 * 512, BF16).rearrange("p (k t) -> p k t", k=16)
            aT = carve(NFF * 512, BF16).rearrange("p (k t) -> p k t", k=NFF)
            GB = carve(D, F32)
            xs = carve(D, F32)
            tmp = carve(D, F32)
            tmp2 = carve(D, F32)
            hbf4 = carve(4 * D, BF16).rearrange("p (a b) -> p a b", a=4)
            sg = [carve(512, F32), carve(512, F32)]
            kvst = [carve(512, BF16), carve(512, BF16)]
            vout = [carve(512, BF16), carve(512, BF16)]
            xbufs = [(xs, "xs"), (tmp2, "tmp1"), (GB, "GB")]
            def xload(ts):
                xb, xk = xbufs[ts % 3]
                I("pool", lambda e: e.dma_start(out=xb, in_=x[T0 + ts * 128:T0 + (ts + 1) * 128, :]), writes=[xk], dma=True)
            for ts in range(3):
                xload(ts)
            for ts in range(4):
                xb, xk = xbufs[ts % 3]
                prenorm(xb, [xk], tmp, "tmp0", hbf4[:, ts, :], "hbf%d" % ts)
                if ts == 0:
                    xload(3)
                transposes_to_hT(hbf4[:, ts, :], "hbf%d" % ts, hT, "hT", ts, 0)
            if dbg and i == 0 and j == 0:
                I("pool", lambda e: e.dma_start(out=dbg_out["d_hT"], in_=hT), reads=["hT"], writes=["dbg"], dma=True)
            ffn_in(w1i, hT, "hT", aT, sg)
            if dbg and i == 0 and j == 0:
                I("pool", lambda e: e.dma_start(out=dbg_out["d_aT"], in_=aT), reads=["aT%d" % c for c in range(NFF)], writes=["dbg"], dma=True)
            load_GB(GB, 1)

            xs2 = hT.rearrange("p k t -> p (k t)")[:, 0:2 * D].bitcast(F32)
            xeb = [(xs, "xs"), (xs2, "hT")]

            def pre_a(pair, T0=T0):
                for tl in range(2):
                    ts = pair * 2 + tl
                    xb, xk = xeb[tl]
                    I("pool", lambda e: e.dma_start(out=xb, in_=x[T0 + ts * 128:T0 + (ts + 1) * 128, :]), writes=[xk], dma=True)

            def epi_a(ts, rs, k, tm, tk, T0=T0, j=j, i=i):
                xb, xk = xeb[ts % 2]
                resid_update(xb, [xk], rs, k, tm, tk)
                sc = SELT[:, 2 * i + j:2 * i + j + 1]
                if j == 0:
                    I("dve", lambda e: e.tensor_scalar(out=X1OWN[:, ts * D:(ts + 1) * D], in0=xb, scalar1=sc, scalar2=1.0,
                                                       op0=ALU.mult, op1=ALU.mult), reads=[xk, "t_sel"], writes=["x1own%d" % ts])
                else:
                    I("dve", lambda e: e.scalar_tensor_tensor(out=X1OWN[:, ts * D:(ts + 1) * D], in0=xb, scalar=sc,
                                                              in1=X1OWN[:, ts * D:(ts + 1) * D], op0=ALU.mult, op1=ALU.add),
                      reads=[xk, "t_sel", "x1own%d" % ts], writes=["x1own%d" % ts])
                prenorm(xb, [xk], tm, tk, hbf4[:, ts, :], "hbf%d" % ts)

            tok_out(lambda k: aT[:, k, :], lambda k: ["aT%d" % k], NFF, w1o, epi_a, GB, [tmp, tmp2], 0.5, pre_pair=pre_a)
            for ts in range(4):
                transposes_to_hT(hbf4[:, ts, :], "hbf%d" % ts, hT, "hT", ts, 1)
            jobs = []
            fm = [(C_KSB + 128 * h, KT_SB[h]) for h in range(8)]
            fm += [(C_KCMP + 128 * g, KT_CMP[0, g]) for g in range(2)] + [(C_VCMP + 128 * g, KT_CMP[1, g]) for g in range(2)]
            fm += [(C_KSLC + 128 * g, KT_SLC[g]) for g in range(2)] + [(C_KWIN + 128 * g, KT_WIN[g]) for g in range(2)]
            for n_, (col0, dst) in enumerate(fm):
                def consume(ps_ap, pskey, n_=n_, dst=dst, T0=T0):
                    st = kvst[n_ % 2]
                    sk_ = "kvst%d" % (n_ % 2)
                    I("act", lambda e: e.copy(out=st, in_=ps_ap), reads=[pskey], writes=[sk_])
                    I("pool", lambda e: e.dma_start(out=dst[:, T0:T0 + 512], in_=st), reads=[sk_], dma=True)
                jobs.append(fm_proj_job(col0, 128, hT, "hT", n_ % 2, consume))
            pipeline(jobs, 2)
            wv = wview(win)
            BANKS_A = [2, 3, 4, 5]
            BANKS_B = [6, 7, 0, 1]
            panels = [("sb", [(C_VSB, 512)], 0, [BANKS_A]), ("sb", [(C_VSB + 512, 512)], 1, [BANKS_B]),
                      ("nsa", [(C_VSLC, 256), (C_VWIN, 256)], 0, [BANKS_A, BANKS_B])]
            jobs = []
            for kind, cols, pidx, bsets in panels:
                for kh in range(2):
                    def load(kind=kind, cols=cols, kh=kh):
                        if kind == "sb":
                            c0, w_ = cols[0]
                            wt, key = stage(wv[:, kh * 8:(kh + 1) * 8, c0:c0 + w_], [128, 8, w_], ("winv", c0, kh))
                            return [(wt, key, w_)]
                        outs_ = stage2([wv[:, kh * 8:(kh + 1) * 8, c0:c0 + w_] for (c0, w_) in cols],
                                       [[128, 8, w_] for (c0, w_) in cols], ("winv2", kh))
                        return [(wt, key, cols[n_][1]) for n_, (wt, key) in enumerate(outs_)]

                    def comp(wts, kind=kind, pidx=pidx, kh=kh, bsets=bsets, T0=T0):
                        o = 0
                        for wi, (wt, key, w_) in enumerate(wts):
                            for ts in range(4):
                                bk = bsets[wi][ts]
                                for kk in range(8):
                                    k = kh * 8 + kk
                                    I("pe", lambda e: e.matmul(PSB[bk][:, o:o + w_], lhsT=hT[:, k, ts * 128:(ts + 1) * 128], rhs=wt[:, kk, :],
                                                               start=(k == 0), stop=(k == 15)), reads=key + ["hT"], writes=[psk[bk]])
                            o += w_
                        if kh == 0:
                            return
                        for ts in range(4):
                            vo = vout[ts % 2]
                            vk = "vout%d" % (ts % 2)
                            kb = (T0 // 128) + ts
                            if kind == "sb":
                                bk = bsets[0][ts]
                                I("act", lambda e: e.copy(out=vo, in_=PSB[bk]), reads=[psk[bk]], writes=[vk])
                                dst = V_SB[4 * pidx:4 * pidx + 4, :, kb, :].rearrange("h p d -> p h d")
                                I("pool", lambda e: e.dma_start(out=dst, in_=vo.rearrange("p (h d) -> p h d", h=4)), reads=[vk], dma=True)
                            else:
                                b0_, b1_ = bsets[0][ts], bsets[1][ts]
                                I("act", lambda e: e.copy(out=vo[:, 0:256], in_=PSB[b0_][:, 0:256]), reads=[psk[b0_]], writes=[vk])
                                I("act", lambda e: e.copy(out=vo[:, 256:512], in_=PSB[b1_][:, 256:512]), reads=[psk[b1_]], writes=[vk])
                                d1 = V_SLC[:, :, kb, :].rearrange("g p d -> p g d")
                                d2 = V_WIN[:, :, kb, :].rearrange("g p d -> p g d")
                                I("pool", lambda e: e.dma_start(out=d1, in_=vo[:, 0:256].rearrange("p (g d) -> p g d", g=2)), reads=[vk], dma=True)
                                I("pool", lambda e: e.dma_start(out=d2, in_=vo[:, 256:512].rearrange("p (g d) -> p g d", g=2)), reads=[vk], dma=True)
                    jobs.append((load, comp))
            pipeline(jobs, 2)

        P.fence()
        rreset()
        H2 = carve(16 * 512, BF16).rearrange("p (k t) -> p k t", k=16)
        OUTS = carve(16 * 512, BF16).rearrange("p (h t) -> p h t", h=16)
        PREFIX = roff[0]
        QT = carve(16 * 512, BF16).rearrange("p (h t) -> p h t", h=16)
        GSIG = carve(512, BF16)
        ATT0 = roff[0]
        tmp = carve(D, F32)
        hbf4 = carve(4 * D, BF16).rearrange("p (a b) -> p a b", a=4)
        if dbg:
            for ts in range(4):
                I("pool", lambda e, ts=ts: e.dma_start(out=dbg_out["d_x1"][i * 512 + ts * 128:i * 512 + (ts + 1) * 128, :],
                                                     in_=X1OWN[:, ts * D:(ts + 1) * D]), reads=["x1own%d" % ts], writes=["dbg"], dma=True)
        for ts in range(4):
            prenorm(X1OWN[:, ts * D:(ts + 1) * D], ["x1own%d" % ts], tmp, "tmp0", hbf4[:, ts, :], "hbf%d" % ts)
            transposes_to_hT(hbf4[:, ts, :], "hbf%d" % ts, H2, "H2", ts, 1)
        jobs = []
        for h in range(16):
            col0 = (C_QSB + 128 * h) if h < 8 else (C_QNSA + 128 * (h - 8))

            def consume(ps_ap, pskey, h=h):
                I("act", lambda e: e.activation(out=QT[:, h, :], in_=ps_ap, func=AF.Copy, scale=float(HD ** -0.5)),
                  reads=[pskey], writes=["qt%d" % h])
            jobs.append(fm_proj_job(col0, 128, H2, "H2", h % 2, consume))

        def consume_gate(ps_ap, pskey):
            eg = tmp[0:24, 0:512]
            I("act", lambda e: e.activation(out=eg, in_=ps_ap, func=AF.Exp, scale=-1.0), reads=[pskey], writes=["tmp0"])
            I("dve", lambda e: e.tensor_scalar(out=eg, in0=eg, scalar1=1.0, scalar2=1.0, op0=ALU.add, op1=ALU.mult),
              reads=["tmp0"], writes=["tmp0"])
            I("dve", lambda e: e.reciprocal(out=GSIG[0:24, :], in_=eg), reads=["tmp0"], writes=["gsig"])
        jobs.append(fm_proj_job(C_GATE, 24, H2, "H2", 2, consume_gate))
        pipeline(jobs, 2)

        P.fence()
        rreset(ATT0)
        KTB = [carve(S, BF16), carve(1536, BF16)]
        VB = [carve(NKB * 128, BF16).rearrange("p (k d) -> p k d", d=128), carve(1536, BF16).rearrange("p (k d) -> p k d", d=128)]
        MSK = carve(12 * 512, BF16).rearrange("p (m t) -> p m t", t=512)
        CSH = carve(512, BF16)
        E1 = [carve(512, F32), carve(512, F32)]
        LB = [carve(512, BF16), carve(512, BF16)]
        LSUM = carve(512, F32)
        LHI = carve(512, BF16)
        LLO = carve(512, BF16)
        WB = [carve(512, BF16), carve(512, BF16)]
        RR = carve(512, F32)
        RG = carve(512, F32)
        ACC = [carve(512, F32) for _ in range(4)]
        PB = [carve(512, BF16) for _ in range(2)]
        PGF = [E1[0], E1[1], LSUM, carve(512, F32)]
        PGK = ["e10", "e11", "lsum", "pgf3"]
        SC = carve(256, F32)
        SC2 = carve(64, F32)
        M8 = carve(16, F32)
        NSEL = carve(64, BF16)
        NSB = [carve(512, BF16), carve(512, BF16)]
        FRC = carve(256, F32)
        CBUF = carve(1040, BF16)
        CX = [carve(64, F32) for _ in range(3)]
        GL = carve(64, BF16)

        I("pool", lambda e: e.dma_start(out=CSH[0:8, :], in_=tin["t_csh"][i]), writes=["csh"], dma=True)
        I("pool", lambda e: e.dma_start(out=FRC, in_=tin["t_frc"][i].rearrange("p a b -> p (a b)")), writes=["frc"], dma=True)

        I("pool", lambda e: e.dma_start(out=MSK[:, 0:8, :], in_=tin["t_sbm"][i]), writes=["msk"], dma=True)
        for h in range(8):
            kt, vb = KTB[0], VB[0]
            hk = nkb // 2
            I("pool", lambda e: e.dma_start(out=kt[:, hk * 128:Nk], in_=KT_SB[h][:, hk * 128:Nk]), writes=["ktb0h"], dma=True)
            I("pool", lambda e: e.dma_start(out=vb[:, hk:nkb, :], in_=V_SB[h][:, hk:nkb, :]), writes=["vb0h"], dma=True)
            I("pool", lambda e: e.dma_start(out=kt[:, 0:hk * 128], in_=KT_SB[h][:, 0:hk * 128]), writes=["ktb0l"], dma=True)
            I("pool", lambda e: e.dma_start(out=vb[:, 0:hk, :], in_=V_SB[h][:, 0:hk, :]), writes=["vb0l"], dma=True)
            I("dve", lambda e: e.memset(LSUM, 0.0), writes=["lsum"])
            ob = 4 + (h % 2)
            kbs = list(range(nkb - 1, -1, -1))
            NS = len(kbs)

            def sbA(n):
                kb = kbs[n]
                diag = kb >= 8 * i
                m = kb - 8 * i
                b1, b2 = n % 2, 2 + (n % 2)
                ktk = "ktb0h" if kb >= hk else "ktb0l"
                I("pe", lambda e: e.matmul(PSB[b1], lhsT=kt[:, kb * 128:(kb + 1) * 128], rhs=QT[:, h, :], start=True, stop=(not diag)),
                  reads=[ktk, "qt%d" % h], writes=[psk[b1]])
                if diag:
                    I("pe", lambda e: e.matmul(PSB[b1], lhsT=IDENT[:, :], rhs=MSK[:, m, :], start=False, stop=True),
                      reads=["t_ident", "msk"], writes=[psk[b1]])
                I("pe", lambda e: e.matmul(PSB[b2], lhsT=kt[:, kb * 128:(kb + 1) * 128], rhs=QT[:, h, :], start=True, stop=False),
                  reads=[ktk, "qt%d" % h], writes=[psk[b2]])
                if diag:
                    I("pe", lambda e: e.matmul(PSB[b2], lhsT=IDENT[:, :], rhs=MSK[:, m, :], start=False, stop=False),
                      reads=["t_ident", "msk"], writes=[psk[b2]])

            def sbB(n):
                b1 = n % 2
                e1, lb = E1[n % 2], LB[n % 2]
                I("act", lambda e: e.activation(out=e1, in_=PSB[b1], func=AF.Exp), reads=[psk[b1]], writes=["e1%d" % (n % 2)])
                I("act", lambda e: e.activation(out=lb, in_=e1, func=AF.Ln, bias=1.0, scale=1.0), reads=["e1%d" % (n % 2)], writes=["lb%d" % (n % 2)])

            def sbC(n):
                b2 = 2 + (n % 2)
                lb = LB[n % 2]
                I("pe", lambda e: e.matmul(PSB[b2], lhsT=NEGTRI[:, :], rhs=lb, start=False, stop=(n == 0)),
                  reads=["t_negtri", "lb%d" % (n % 2)], writes=[psk[b2]])
                if n > 0:
                    I("pe", lambda e: e.matmul(PSB[b2], lhsT=NEGONES[:, :], rhs=LHI, start=False, stop=False),
                      reads=["t_negones", "lhi"], writes=[psk[b2]])
                    I("pe", lambda e: e.matmul(PSB[b2], lhsT=NEGONES[:, :], rhs=LLO, start=False, stop=True),
                      reads=["t_negones", "llo"], writes=[psk[b2]])

            def sbD(n):
                b2 = 2 + (n % 2)
                lb, wb = LB[n % 2], WB[n % 2]
                I("act", lambda e: e.activation(out=wb, in_=PSB[b2], func=AF.Exp), reads=[psk[b2]], writes=["wb%d" % (n % 2)])
                I("dve", lambda e: e.tensor_tensor(out=LSUM, in0=LSUM, in1=lb, op=ALU.add), reads=["lsum", "lb%d" % (n % 2)], writes=["lsum"])
                if n + 1 < NS:
                    I("dve", lambda e: e.tensor_copy(out=LHI, in_=LSUM), reads=["lsum"], writes=["lhi"])
                    I("dve", lambda e: e.tensor_tensor(out=LLO, in0=LSUM, in1=LHI, op=ALU.subtract), reads=["lsum", "lhi"], writes=["llo"])

            def sbE(n):
                kb = kbs[n]
                wb = WB[n % 2]
                vbk = "vb0h" if kb >= hk else "vb0l"
                I("pe", lambda e: e.matmul(PSB[ob], lhsT=vb[:, kb, :], rhs=wb, start=(n == 0), stop=(n == NS - 1)),
                  reads=[vbk, "wb%d" % (n % 2)], writes=[psk[ob]])

            sbA(0)
            sbB(0)
            for n in range(NS):
                if n + 1 < NS:
                    sbA(n + 1)
                    sbB(n + 1)
                sbC(n)
                sbD(n)
                if n >= 1:
                    sbE(n - 1)
            sbE(NS - 1)
            I("dve", lambda e: e.tensor_copy(out=OUTS[:, h, :], in_=PSB[ob]), reads=[psk[ob]], writes=["outs%d" % h])

        for kv in range(2):
            w1t, w1key = stage(cw1[kv].rearrange("(l d) o -> d l o", d=128), [128, 32, 128], ("cw1", kv))
            if i == 0:
                pt = E1[0][:, 0:32]
                I("pool", lambda e, kv=kv, pt=pt: e.dma_start(out=pt, in_=cpos[kv].rearrange("l d -> d l"), allow_slow_non_contiguous=True),
                  writes=["e10"], dma=True)
                ptb = PB[0][:, 0:32]
                I("dve", lambda e, pt=pt, ptb=ptb: e.tensor_copy(out=ptb, in_=pt), reads=["e10"], writes=["pb0"])
                for l in range(32):
                    I("pe", lambda e, l=l, ptb=ptb, w1t=w1t: e.matmul(PSB[6][:, 0:1], lhsT=w1t[:, l, :], rhs=ptb[:, l:l + 1], start=(l == 0), stop=(l == 31)),
                      reads=w1key + ["pb0"], writes=[psk[6]])
                I("dve", lambda e, kv=kv: e.tensor_copy(out=POSB[:, kv:kv + 1], in_=PSB[6][:, 0:1]), reads=[psk[6]], writes=["posb"])
            for g in range(2):
                if i == 0:
                    I("dve", lambda e: e.memset(CBUF[:, 0:16], 0.0), writes=["cbuf"])
                    I("pool", lambda e, kv=kv, g=g: e.dma_start(out=CBUF[:, 16:1040], in_=KT_CMP[kv, g][:, 0:1024]), writes=["cbuf"], dma=True)
                else:
                    I("pool", lambda e, kv=kv, g=g: e.dma_start(out=CBUF[:, 0:1040], in_=KT_CMP[kv, g][:, 1024 * i - 16:1024 * i + 1024]),
                      writes=["cbuf"], dma=True)
                for l in range(32):
                    I("pe", lambda e, l=l, w1t=w1t: e.matmul(PSB[6][:, 0:64], lhsT=w1t[:, l, :], rhs=CBUF[:, l:l + 1009:16], start=(l == 0), stop=(l == 31)),
                      reads=w1key + ["cbuf"], writes=[psk[6]])
                xx, x2, uu = CX
                I("dve", lambda e, kv=kv: e.tensor_scalar(out=xx, in0=PSB[6][:, 0:64], scalar1=POSB[:, kv:kv + 1], scalar2=1.0, op0=ALU.add, op1=ALU.mult),
                  reads=[psk[6], "posb"], writes=["cx0"])
                I("dve", lambda e: e.tensor_tensor(out=x2, in0=xx, in1=xx, op=ALU.mult), reads=["cx0"], writes=["cx1"])
                I("dve", lambda e: e.tensor_scalar(out=x2, in0=x2, scalar1=0.044715, scalar2=1.0, op0=ALU.mult, op1=ALU.add), reads=["cx1"], writes=["cx1"])
                I("dve", lambda e: e.tensor_tensor(out=uu, in0=x2, in1=xx, op=ALU.mult), reads=["cx0", "cx1"], writes=["cx2"])
                I("act", lambda e: e.activation(out=uu, in_=uu, func=AF.Exp, scale=float(-2.0 * np.sqrt(2.0 / np.pi))), reads=["cx2"], writes=["cx2"])
                I("dve", lambda e: e.tensor_scalar(out=uu, in0=uu, scalar1=1.0, scalar2=1.0, op0=ALU.add, op1=ALU.mult), reads=["cx2"], writes=["cx2"])
                I("dve", lambda e: e.reciprocal(out=uu, in_=uu), reads=["cx2"], writes=["cx2"])
                I("dve", lambda e: e.tensor_tensor(out=GL, in0=xx, in1=uu, op=ALU.mult), reads=["cx0", "cx2"], writes=["gl"])
                if kv == 0:
                    I("pe", lambda e: e.matmul(PSB[6][:, 64:128], lhsT=CW2[:, 0:128], rhs=GL, start=True, stop=True), reads=["cw2", "gl"], writes=[psk[6]])
                    I("dve", lambda e, g=g: e.tensor_copy(out=KC[:, (g * NSLOT + i) * 64:(g * NSLOT + i + 1) * 64], in_=PSB[6][:, 64:128]),
                      reads=[psk[6]], writes=["kc"])
                else:
                    I("pe", lambda e: e.matmul(PSB[6][0:64, 128:256], lhsT=GL, rhs=CW2[:, 128:256], start=True, stop=True), reads=["cw2", "gl"], writes=[psk[6]])
                    I("dve", lambda e, g=g: e.tensor_copy(out=VC[0:64, (g * NSLOT + i) * 128:(g * NSLOT + i + 1) * 128], in_=PSB[6][0:64, 128:256]),
                      reads=[psk[6]], writes=["vc"])

        def gate_rg(branch, h, den_bank):
            I("dve", lambda e: e.tensor_scalar(out=RR, in0=PSB[den_bank], scalar1=1e-30, scalar2=1.0, op0=ALU.add, op1=ALU.mult),
              reads=[psk[den_bank]], writes=["rr"])
            I("dve", lambda e: e.reciprocal(out=RR, in_=RR), reads=["rr"], writes=["rr"])
            col = branch * 8 + h
            I("pe", lambda e: e.matmul(PSB[6], lhsT=SELG[0:24, col * 128:(col + 1) * 128], rhs=GSIG[0:24, :], start=True, stop=True),
              reads=["t_selg", "gsig"], writes=[psk[6]])
            I("dve", lambda e: e.tensor_tensor(out=RG, in0=RR, in1=PSB[6], op=ALU.mult), reads=["rr", psk[6]], writes=["rg"])

        def acc_branch(r, o_bank, firstb):
            if firstb:
                I("dve", lambda e: e.tensor_tensor(out=ACC[r], in0=PSB[o_bank], in1=RG, op=ALU.mult), reads=[psk[o_bank], "rg"], writes=["acc%d" % r])
            else:
                I("dve", lambda e: e.tensor_tensor(out=RG, in0=PSB[o_bank], in1=RG, op=ALU.mult), reads=[psk[o_bank], "rg"], writes=["rg"])
                I("dve", lambda e: e.tensor_tensor(out=ACC[r], in0=ACC[r], in1=RG, op=ALU.add), reads=["rg", "acc%d" % r], writes=["acc%d" % r])

        for g in range(2):
            I("pool", lambda e: e.dma_start(out=MSK[0:64, 0:NSLOT, :], in_=tin["t_cpm"][i]), writes=["msk"], dma=True)
            for r in range(4):
                h = 4 * g + r
                for gp in range(i + 1):
                    b = gp % 2
                    eb = WB[gp % 2] if gp < 2 else LB[gp % 2]
                    ebk = ("wb%d" if gp < 2 else "lb%d") % (gp % 2)
                    I("pe", lambda e, gp=gp, b=b, h=h: e.matmul(PSB[b][0:64, :], lhsT=KC[:, (g * NSLOT + gp) * 64:(g * NSLOT + gp + 1) * 64],
                                                             rhs=QT[:, 8 + h, :], start=True, stop=False),
                      reads=["kc", "qt%d" % (8 + h)], writes=[psk[b]])
                    I("pe", lambda e, b=b, h=h: e.matmul(PSB[b][0:64, :], lhsT=SELH[0:8, h * 128:h * 128 + 64], rhs=CSH[0:8, :], start=False, stop=False),
                      reads=["t_selh", "csh"], writes=[psk[b]])
                    I("pe", lambda e, b=b, gp=gp: e.matmul(PSB[b][0:64, :], lhsT=IDENT[0:64, 0:64], rhs=MSK[0:64, gp, :], start=False, stop=True),
                      reads=["t_ident", "msk"], writes=[psk[b]])
                    I("act", lambda e, b=b, gp=gp, h=h, eb=eb: e.activation(out=eb[0:64, :], in_=PSB[b][0:64, :], func=AF.Exp,
                                                                          bias=BCC[0:64, h * NSLOT + (i - gp):h * NSLOT + (i - gp) + 1], scale=1.0),
                      reads=[psk[b], "t_bcc"], writes=[ebk])
                    I("pe", lambda e, gp=gp, eb=eb: e.matmul(PSB[3], lhsT=ONESB[0:64, :], rhs=eb[0:64, :], start=(gp == 0), stop=(gp == i)),
                      reads=["t_ones", ebk], writes=[psk[3]])
                    I("pe", lambda e, gp=gp, eb=eb: e.matmul(PSB[2], lhsT=VC[0:64, (g * NSLOT + gp) * 128:(g * NSLOT + gp + 1) * 128], rhs=eb[0:64, :],
                                                          start=(gp == 0), stop=(gp == i)),
                      reads=["vc", ebk], writes=[psk[2]])
                gate_rg(0, h, 3)
                acc_branch(r, 2, True)
                for gp in range(i + 1):
                    eb = WB[gp % 2] if gp < 2 else LB[gp % 2]
                    ebk = ("wb%d" if gp < 2 else "lb%d") % (gp % 2)
                    pg, pgk = PGF[gp], PGK[gp]
                    if r == 0:
                        I("dve", lambda e, eb=eb, pg=pg: e.tensor_tensor(out=pg[0:64, :], in0=eb[0:64, :], in1=RR[0:64, :], op=ALU.mult),
                          reads=[ebk, "rr"], writes=[pgk])
                    else:
                        I("dve", lambda e, eb=eb: e.tensor_tensor(out=RG[0:64, :], in0=eb[0:64, :], in1=RR[0:64, :], op=ALU.mult),
                          reads=[ebk, "rr"], writes=["rg"])
                        I("dve", lambda e, pg=pg: e.tensor_tensor(out=pg[0:64, :], in0=pg[0:64, :], in1=RG[0:64, :], op=ALU.add),
                          reads=["rg", pgk], writes=[pgk])
            bgs = [(PB[0], "pb0"), (PB[1], "pb1"), (LB[0], "lb0"), (LB[1], "lb1")]
            for gp in range(i + 1):
                I("dve", lambda e, gp=gp: e.tensor_copy(out=bgs[gp][0][0:64, :], in_=PGF[gp][0:64, :]), reads=[PGK[gp]], writes=[bgs[gp][1]])
            for ts in range(4):
                for gp in range(i + 1):
                    I("pe", lambda e, ts=ts, gp=gp: e.matmul(PSB[5][:, ts * 64:(ts + 1) * 64], lhsT=bgs[gp][0][0:64, ts * 128:(ts + 1) * 128],
                                                         rhs=MSLC[0:64, gp * 64:(gp + 1) * 64], start=(gp == 0), stop=(gp == i)),
                      reads=[bgs[gp][1], "t_mslc"], writes=[psk[5]])
            I("dve", lambda e: e.tensor_tensor(out=SC, in0=PSB[5][:, 0:256], in1=FRC, op=ALU.add), reads=[psk[5], "frc"], writes=["sc"])
            for ts in range(4):
                scs = SC[:, ts * 64:(ts + 1) * 64]
                I("dve", lambda e, scs=scs: e.max(out=M8[:, 0:8], in_=scs), reads=["sc"], writes=["m8"])
                I("dve", lambda e, scs=scs: e.match_replace(out=SC2, in_to_replace=M8[:, 0:8], in_values=scs, imm_value=-3.0e38),
                  reads=["sc", "m8"], writes=["sc2"])
                I("dve", lambda e: e.max(out=M8[:, 8:16], in_=SC2), reads=["sc2"], writes=["m8"])
                I("dve", lambda e, scs=scs: e.tensor_scalar(out=NSEL, in0=scs, scalar1=M8[:, 15:16], scalar2=1.0, op0=ALU.is_lt, op1=ALU.mult),
                  reads=["sc", "m8"], writes=["nsel"])
                I("pe", lambda e, ts=ts: e.transpose(out=PSBF[0:64, ts * 128:(ts + 1) * 128], in_=NSEL, identity=IDENT[:, :]),
                  reads=["nsel", "t_ident"], writes=[psk[7]])
            I("dve", lambda e: e.tensor_copy(out=NSB[0][0:64, :], in_=PSBF[0:64, 0:512]), reads=[psk[7]], writes=["ns0"])
            I("dve", lambda e: e.tensor_copy(out=NSB[1][0:64, :], in_=PSBF[0:64, 0:512]), reads=[psk[7]], writes=["ns1"])
            I("pool", lambda e: e.dma_start(out=NSB[0][64:65, :], in_=tin["t_csh"][i][4 * g:4 * g + 1, :]), writes=["nsr0"], dma=True)
            kt, vb = KTB[0], VB[0]
            I("pool", lambda e, kt=kt: e.dma_start(out=kt[:, 0:Nk], in_=KT_SLC[g][:, 0:Nk]), writes=["ktb0h", "ktb0l"], dma=True)
            I("pool", lambda e, vb=vb: e.dma_start(out=vb[:, 0:nkb, :], in_=V_SLC[g][:, 0:nkb, :]), writes=["vb0h", "vb0l"], dma=True)
            I("pool", lambda e: e.dma_start(out=MSK[:, 0:8, :], in_=tin["t_slm"][i]), writes=["msk"], dma=True)
            for r in range(4):
                h = 4 * g + r
                nsb = NSB[r % 2]
                nsk = ["ns%d" % (r % 2), "nsr%d" % (r % 2)]
                if r < 3:
                    I("pool", lambda e: e.dma_start(out=NSB[(r + 1) % 2][64:65, :], in_=tin["t_csh"][i][h + 1:h + 2, :]),
                      writes=["nsr%d" % ((r + 1) % 2)], dma=True)

                def slA(kb):
                    diag = kb >= 8 * i
                    m = kb - 8 * i
                    bb = kb % 2
                    I("pe", lambda e: e.matmul(PSB[bb], lhsT=kt[:, kb * 128:(kb + 1) * 128], rhs=QT[:, 8 + h, :], start=True, stop=False),
                      reads=["ktb0h", "ktb0l", "qt%d" % (8 + h)], writes=[psk[bb]])
                    I("pe", lambda e: e.matmul(PSB[bb], lhsT=NEGEXP[0:65, kb * 128:(kb + 1) * 128], rhs=nsb[0:65, :], start=False, stop=(not diag)),
                      reads=["t_negexp"] + nsk, writes=[psk[bb]])
                    if diag:
                        I("pe", lambda e: e.matmul(PSB[bb], lhsT=IDENT[:, :], rhs=MSK[:, m, :], start=False, stop=True),
                          reads=["t_ident", "msk"], writes=[psk[bb]])

                def slB(kb):
                    bb = kb % 2
                    wb = WB[kb % 2]
                    rel = kb - 8 * i + OFFS
                    I("act", lambda e: e.activation(out=wb, in_=PSB[bb], func=AF.Exp, bias=BC[:, h * NREL + rel:h * NREL + rel + 1], scale=1.0),
                      reads=[psk[bb], "t_bc"], writes=["wb%d" % (kb % 2)])

                ob_, db_ = (2, 3) if r % 2 == 0 else (4, 5)

                def slC(kb):
                    wb = WB[kb % 2]
                    I("pe", lambda e: e.matmul(PSB[ob_], lhsT=vb[:, kb, :], rhs=wb, start=(kb == 0), stop=(kb == nkb - 1)),
                      reads=["vb0h", "vb0l", "wb%d" % (kb % 2)], writes=[psk[ob_]])
                    if kb == 0:
                        I("dve", lambda e: e.tensor_copy(out=E1[0], in_=wb), reads=["wb%d" % (kb % 2)], writes=["e10"])
                    else:
                        I("dve", lambda e: e.tensor_tensor(out=E1[0], in0=E1[0], in1=wb, op=ALU.add), reads=["e10", "wb%d" % (kb % 2)], writes=["e10"])

                slA(0)
                slB(0)
                for kb in range(nkb):
                    if kb + 1 < nkb:
                        slA(kb + 1)
                        slB(kb + 1)
                    slC(kb)
                I("pe", lambda e: e.matmul(PSB[db_], lhsT=ONESF[:, :], rhs=E1[0], start=True, stop=True), reads=["t_onesf", "e10"], writes=[psk[db_]])
                gate_rg(1, h, db_)
                acc_branch(r, ob_, False)
            kb_lo = max(8 * i - 4, 0)
            nwin = nkb - kb_lo
            kt, vb = KTB[1], VB[1]
            I("pool", lambda e, kt=kt: e.dma_start(out=kt[:, 0:nwin * 128], in_=KT_WIN[g][:, kb_lo * 128:nkb * 128]), writes=["ktb1"], dma=True)
            I("pool", lambda e, vb=vb: e.dma_start(out=vb[:, 0:nwin, :], in_=V_WIN[g][:, kb_lo:nkb, :]), writes=["vb1"], dma=True)
            I("pool", lambda e: e.dma_start(out=MSK[:, 0:12, :], in_=tin["t_wnm"][i]), writes=["msk"], dma=True)
            for r in range(4):
                h = 4 * g + r

                def wnA(n_):
                    kb = kb_lo + n_
                    m = kb - (8 * i - 4)
                    bb = n_ % 2
                    I("pe", lambda e: e.matmul(PSB[bb], lhsT=kt[:, n_ * 128:(n_ + 1) * 128], rhs=QT[:, 8 + h, :], start=True, stop=False),
                      reads=["ktb1", "qt%d" % (8 + h)], writes=[psk[bb]])
                    I("pe", lambda e: e.matmul(PSB[bb], lhsT=SELH[0:8, h * 128:(h + 1) * 128], rhs=CSH[0:8, :], start=False, stop=False),
                      reads=["t_selh", "csh"], writes=[psk[bb]])
                    I("pe", lambda e: e.matmul(PSB[bb], lhsT=IDENT[:, :], rhs=MSK[:, m, :], start=False, stop=True),
                      reads=["t_ident", "msk"], writes=[psk[bb]])

                def wnB(n_):
                    kb = kb_lo + n_
                    bb = n_ % 2
                    wb = WB[n_ % 2]
                    rel = kb - 8 * i + OFFS
                    I("act", lambda e: e.activation(out=wb, in_=PSB[bb], func=AF.Exp, bias=BC[:, h * NREL + rel:h * NREL + rel + 1], scale=1.0),
                      reads=[psk[bb], "t_bc"], writes=["wb%d" % (n_ % 2)])

                ob_, db_ = (2, 3) if r % 2 == 0 else (4, 5)

                def wnC(n_):
                    wb = WB[n_ % 2]
                    I("pe", lambda e: e.matmul(PSB[ob_], lhsT=vb[:, n_, :], rhs=wb, start=(n_ == 0), stop=(n_ == nwin - 1)),
                      reads=["vb1", "wb%d" % (n_ % 2)], writes=[psk[ob_]])
                    if n_ == 0:
                        I("dve", lambda e: e.tensor_copy(out=E1[0], in_=wb), reads=["wb%d" % (n_ % 2)], writes=["e10"])
                    else:
                        I("dve", lambda e: e.tensor_tensor(out=E1[0], in0=E1[0], in1=wb, op=ALU.add), reads=["e10", "wb%d" % (n_ % 2)], writes=["e10"])

                wnA(0)
                wnB(0)
                for n_ in range(nwin):
                    if n_ + 1 < nwin:
                        wnA(n_ + 1)
                        wnB(n_ + 1)
                    wnC(n_)
                I("pe", lambda e: e.matmul(PSB[db_], lhsT=ONESF[:, :], rhs=E1[0], start=True, stop=True), reads=["t_onesf", "e10"], writes=[psk[db_]])
                gate_rg(2, h, db_)
                acc_branch(r, ob_, False)
                I("act", lambda e, r=r, h=h: e.copy(out=OUTS[:, 8 + h, :], in_=ACC[r]), reads=["acc%d" % r], writes=["outs%d" % (8 + h)])
        if dbg:
            I("pool", lambda e: e.dma_start(out=dbg_out["d_sb"][i].rearrange("h p t -> p h t"), in_=OUTS[:, 0:8, :]),
              reads=["outs%d" % h for h in range(8)], writes=["dbg"], dma=True)
            I("pool", lambda e: e.dma_start(out=dbg_out["d_nsa"][i].rearrange("h p t -> p h t"), in_=OUTS[:, 8:16, :]),
              reads=["outs%d" % h for h in range(8, 16)], writes=["dbg"], dma=True)

        P.fence()
        rreset(PREFIX)
        hT = carve(16 * 512, BF16).rearrange("p (k t) -> p k t", k=16)
        GB = carve(D, F32)
        tmp = carve(D, F32)
        tmp2 = carve(D, F32)
        sg = [carve(512, F32), carve(512, F32)]
        FFN_A0 = roff[0]
        MG = carve(16 * 512, BF16).rearrange("p (k t) -> p k t", k=16)
        hbf4 = carve(4 * D, BF16).rearrange("p (a b) -> p a b", a=4)
        GT = [carve(512, F32) for _ in range(4)]
        load_GB(GB, 3)
        wv = wview(win)
        jobs = []
        for c in range(16):
            def load(c=c):
                h1 = stage2([wbsb.rearrange("(h p) c -> p h c", p=128)[:, :, c * 128:(c + 1) * 128],
                             wbnsa.rearrange("(h p) c -> p h c", p=128)[:, :, c * 128:(c + 1) * 128]],
                            [[128, 8, 128], [128, 8, 128]], ("wb", c))
                h2 = stage2([wv[:, :, C_MERGE + c * 128:C_MERGE + (c + 1) * 128],
                             wv[:, :, C_MERGE + 2048 + c * 128:C_MERGE + 2048 + (c + 1) * 128]],
                            [[128, 16, 128], [128, 16, 128]], ("wm", c))
                return h1, h2

            def comp(hh, c=c):
                ((wsb, ksb_), (wns, kns_)), ((wm0, km0), (wm1, km1)) = hh
                b0 = (c % 2) * 4
                for hd in range(8):
                    I("pe", lambda e, hd=hd, wsb=wsb, b0=b0: e.matmul(PSB[b0], lhsT=wsb[:, hd, :], rhs=OUTS[:, hd, :], start=(hd == 0), stop=(hd == 7)),
                      reads=ksb_ + ["outs%d" % hd], writes=[psk[b0]])
                for hd in range(8):
                    I("pe", lambda e, hd=hd, wns=wns, b0=b0: e.matmul(PSB[b0 + 1], lhsT=wns[:, hd, :], rhs=OUTS[:, 8 + hd, :], start=(hd == 0), stop=(hd == 7)),
                      reads=kns_ + ["outs%d" % (8 + hd)], writes=[psk[b0 + 1]])
                for jj, (wm, kmm) in enumerate(((wm0, km0), (wm1, km1))):
                    for k in range(16):
                        I("pe", lambda e, k=k, wm=wm, b=b0 + 2 + jj: e.matmul(PSB[b], lhsT=wm[:, k, :], rhs=H2[:, k, :], start=(k == 0), stop=(k == 15)),
                          reads=kmm + ["H2"], writes=[psk[b0 + 2 + jj]])
                for jj in range(2):
                    I("act", lambda e, jj=jj, b=b0 + 2 + jj: e.activation(out=GT[jj], in_=PSB[b], func=AF.Sigmoid), reads=[psk[b0 + 2 + jj]], writes=["gt%d" % jj])
                    I("dve", lambda e, jj=jj, b=b0 + jj: e.tensor_tensor(out=GT[2 + jj], in0=GT[jj], in1=PSB[b], op=ALU.mult),
                      reads=["gt%d" % jj, psk[b0 + jj]], writes=["gt%d" % (2 + jj)])
                I("dve", lambda e, c=c: e.tensor_tensor(out=MG[:, c, :], in0=GT[2], in1=GT[3], op=ALU.add), reads=["gt2", "gt3"], writes=["mg%d" % c])
            jobs.append((load, comp))
        pipeline(jobs, 1)

        def epi_post(ts, rs, k, tm, tk):
            xr = X1OWN[:, ts * D:(ts + 1) * D]
            resid_update(xr, ["x1own%d" % ts], rs, k, tm, tk)
            prenorm(xr, ["x1own%d" % ts], tm, tk, hbf4[:, ts, :], "hbf%d" % ts)

        tok_out(lambda k: MG[:, k, :], lambda k: ["mg%d" % k], 16, wout, epi_post, GB, [tmp, tmp2], 1.0)
        if dbg:
            for ts in range(4):
                I("pool", lambda e, ts=ts: e.dma_start(out=dbg_out["d_x2"][i * 512 + ts * 128:i * 512 + (ts + 1) * 128, :],
                                                     in_=X1OWN[:, ts * D:(ts + 1) * D]), reads=["x1own%d" % ts], writes=["dbg"], dma=True)
        for ts in range(4):
            transposes_to_hT(hbf4[:, ts, :], "hbf%d" % ts, hT, "hT", ts, 2)
        P.fence()
        rreset(FFN_A0)
        aT = carve(NFF * 512, BF16).rearrange("p (k t) -> p k t", k=NFF)
        ffn_in(w2i, hT, "hT", aT, sg)
        load_GB(GB, 5)

        def epi_f2(ts, rs, k, tm, tk):
            xr = X1OWN[:, ts * D:(ts + 1) * D]
            resid_update(xr, ["x1own%d" % ts], rs, k, tm, tk)
            I("pool", lambda e: e.dma_start(out=out[i * 512 + ts * 128:i * 512 + (ts + 1) * 128, :], in_=xr),
              reads=["x1own%d" % ts], dma=True)

        tok_out(lambda k: aT[:, k, :], lambda k: ["aT%d" % k], NFF, w2o, epi_f2, GB, [tmp, tmp2], 0.5)

    P.finish("sp")
    P.emit()
    return nc


_CACHE = {}


def _run(S, inputs, dbg=False):
    B = inputs["x"].shape[0]
    NSLOT = S // 1024
    key = (S, dbg)
    if key not in _CACHE:
        _CACHE[key] = build_program(S, dbg)
    nc = _CACHE[key]
    f = lambda a: np.ascontiguousarray(np.asarray(a, np.float32))
    gains = np.stack([f(inputs[k])[0] for k in ["ffn1_pre_g", "ffn1_post_g", "mix_pre_g", "mix_post_g", "ffn2_pre_g", "ffn2_post_g"]])
    shared = {"gains": gains}
    for k in ["ffn1_w_in", "ffn1_w_out", "ffn2_w_in", "ffn2_w_out", "w_in", "cmp_k_w1", "cmp_v_w1", "cmp_k_w2", "cmp_v_w2",
              "cmp_pos_k", "cmp_pos_v", "w_branch_sb", "w_branch_nsa", "w_out"]:
        shared[k] = f(inputs[k])[0]
    tabs = [make_tables(S, c) for c in range(2)]
    xin = f(inputs["x"])
    in_maps = []
    ncores = 2 * B
    for core in range(ncores):
        b, c = core // 2, core % 2
        m = dict(shared)
        m["x"] = xin[b]
        m.update(tabs[c])
        in_maps.append(m)
    res = run_bass_kernel_spmd(nc, in_maps, core_ids=list(range(ncores)))
    outp = np.zeros((B, S, D), np.float32)
    for core in range(ncores):
        b, c = core // 2, core % 2
        o = res.results[core]["out"]
        for i in range(NSLOT):
            t0 = 1024 * i + 512 * c
            outp[b, t0:t0 + 512] = o[i * 512:(i + 1) * 512]
    return outp, res


def kernel(**inputs):
    S = inputs["x"].shape[1]
    outp, _ = _run(S, inputs)
    return outp
```

```python
import contextlib
import numpy as np
import ml_dtypes
import concourse.bass as bass
import concourse.mybir as mybir
from concourse.bass_utils import run_bass_kernel_spmd

F32 = mybir.dt.float32
BF16 = mybir.dt.bfloat16
AF = mybir.ActivationFunctionType
ALU = mybir.AluOpType

D = 2048
DFF = 5632
NCH = 16
NFF = 44
HD = 128
NEG = -30000.0
NO_SEL = False
ENGS = ["pe", "act", "dve", "pool", "sp"]
N_DMA_SEMS = 8
SLOPES = [2.0 ** (-(h + 1)) for h in range(8)]
C_QSB, C_KSB, C_VSB, C_QNSA = 0, 1024, 2048, 3072
C_KCMP, C_VCMP, C_KSLC, C_VSLC, C_KWIN, C_VWIN = 4096, 4352, 4608, 4864, 5120, 5376
C_GATE, C_MERGE = 5632, 5656
INCOLS = 9752


class Tok:
    __slots__ = ("sk", "idx", "needed", "val")

    def __init__(self, sk, idx):
        self.sk, self.idx, self.needed, self.val = sk, idx, False, None


class _Rec:
    def __init__(self):
        self.call = None

    def __getattr__(self, name):
        def f(*args, **kwargs):
            self.call = (name, args, kwargs)
            return None
        return f


def _freeze(fn):
    rec = _Rec()
    fn(rec)
    name, args, kwargs = rec.call
    return lambda engine: getattr(engine, name)(*args, **kwargs)


class Prog:
    def __init__(self, nc):
        self.nc = nc
        self.q = {e: [] for e in ENGS}
        self.cnt = {}
        self.last_w = {}
        self.readers = {}
        self.waited = {e: {} for e in ENGS}
        self.dma_rr = {e: 0 for e in ENGS}
        self.dma_last = {}
        self.all_tokens = {}
        self.pending = {e: [] for e in ENGS}

    def _newtok(self, sk):
        i = self.cnt.get(sk, 0) + 1
        self.cnt[sk] = i
        t = Tok(sk, i)
        self.all_tokens.setdefault(sk, []).append(t)
        return t

    def write_deps(self, keys):
        deps = []
        for w in keys:
            t = self.last_w.get(w)
            if t is not None:
                deps.append(t)
            deps.extend(self.readers.get(w, {}).values())
        return deps

    @staticmethod
    def _expand(keys):
        out = []
        for k in keys:
            if k == "ps7":
                out += ["ps7a", "ps7b"]
            else:
                out.append(k)
        return out

    def issue(self, eng, fn, reads=(), writes=(), dma=False, pre_deps=None, dgroup=None):
        fn = _freeze(fn)
        reads = self._expand(reads)
        writes = self._expand(writes)
        deps = list(self.pending[eng])
        self.pending[eng] = []
        for r in reads:
            t = self.last_w.get(r)
            if t is not None:
                deps.append(t)
        if pre_deps is not None:
            deps.extend(pre_deps)
        else:
            deps.extend(self.write_deps(writes))
        if dma:
            grp = dgroup or eng
            j = self.dma_rr.get(grp, 0)
            self.dma_rr[grp] = (j + 1) % N_DMA_SEMS
            sk = "D_%s_%d" % (grp, j)
            prev = self.dma_last.get(sk)
            if prev is not None:
                deps.append(prev)
            tok = self._newtok(sk)
            self.dma_last[sk] = tok
        else:
            tok = self._newtok("E_" + eng)
        waits = []
        wd = self.waited[eng]
        for t in deps:
            if eng == "pe" and t.sk == "E_pe":
                continue
            if wd.get(t.sk, 0) >= t.idx:
                continue
            wd[t.sk] = t.idx
            t.needed = True
            waits.append(t)
        self.q[eng].append((waits, fn, tok, dma))
        for r in reads:
            self.readers.setdefault(r, {})[tok.sk] = tok
        for w in writes:
            self.last_w[w] = tok
            self.readers[w] = {}
        return tok

    def fence(self, engs=("pe", "act", "dve", "pool")):
        toks = [ts[-1] for sk, ts in self.all_tokens.items() if not (sk.startswith("D_sp") or sk.startswith("D_pk"))]
        for e in engs:
            self.pending[e] = list(toks)

    def finish(self, eng="sp"):
        waits = []
        for ts in self.all_tokens.values():
            ts[-1].needed = True
            waits.append(ts[-1])
        self.q[eng].append((waits, None, None, False))

    def emit(self):
        nc = self.nc
        for sk, toks in self.all_tokens.items():
            v = 0
            step = 16 if sk.startswith("D_") else 1
            for t in toks:
                if t.needed:
                    v += step
                    t.val = v
        with contextlib.ExitStack() as es:
            es.enter_context(nc.allow_low_precision("bf16 matmul operands by design; fp32 PSUM accumulation"))
            sems = {sk: es.enter_context(nc.semaphore(sk)) for sk in self.all_tokens}
            block = es.enter_context(nc.Block())
            engmap = {"pe": block.tensor, "act": block.scalar, "dve": block.vector,
                      "pool": block.gpsimd, "sp": block.sync}

            def mk(e):
                lst = self.q[e]

                def body(engine):
                    for waits, fn, tok, dma in lst:
                        for t in waits:
                            engine.wait_ge(sems[t.sk], t.val)
                        if fn is None:
                            continue
                        ins = fn(engine)
                        if tok.needed:
                            ins.then_inc(sems[tok.sk], 16 if dma else 1)
                return body

            for e in ENGS:
                if self.q[e]:
                    engmap[e](mk(e))


def bf(a):
    return np.asarray(a, np.float32).astype(ml_dtypes.bfloat16)


def make_tables(S, c):
    NSLOT = S // 1024
    NKB = S // 128
    T = {}
    sel = np.zeros((128, 2 * NSLOT), np.float32)
    for i in range(NSLOT):
        sel[:, 2 * i + c] = 1.0
    T["t_sel"] = sel
    p = np.arange(128)[:, None]
    t = np.arange(512)[None, :]
    sbm = np.zeros((NSLOT, 128, 8, 512), np.float32)
    slm = np.zeros((NSLOT, 128, 8, 512), np.float32)
    wnm = np.zeros((NSLOT, 128, 12, 512), np.float32)
    cpm = np.zeros((NSLOT, 64, NSLOT, 512), np.float32)
    csh = np.zeros((NSLOT, 8, 512), np.float32)
    frc = np.zeros((NSLOT, 128, 4, 64), np.float32)
    n_cmp = (S - 32) // 16 + 1
    for i in range(NSLOT):
        q = 1024 * i + 512 * c + t
        for m in range(8):
            k = 1024 * i + 128 * m + p
            sbm[i, :, m, :] = np.where(k < q, 0.0, NEG)
            slm[i, :, m, :] = np.where(k <= q, 0.0, NEG)
        for m in range(12):
            k = 1024 * i - 512 + 128 * m + p
            ok = (q - k >= 0) & (q - k < 512) & (k >= 0)
            wnm[i, :, m, :] = np.where(ok, 0.0, NEG)
        for g in range(NSLOT):
            n = 64 * g - 1 + np.arange(64)[:, None]
            ok = (n >= 0) & (n < n_cmp) & (16 * n + 31 <= q)
            cpm[i, :, g, :] = np.where(ok, 0.0, NEG)
        for h in range(8):
            csh[i, h, :] = -SLOPES[h] * (512 * c + np.arange(512))
        for ts in range(4):
            tq = 1024 * i + 512 * c + 128 * ts + np.arange(128)[:, None]
            blk = np.arange(64)[None, :]
            cur = tq // 64
            valid = blk * 64 <= tq
            f = np.where(blk == 0, 1e9, 0.0)
            f = np.where(blk == cur - 1, 2e9, f)
            f = np.where(blk == cur, 3e9, f)
            f = np.where(valid, f, -1e30)
            frc[i, :, ts, :] = f
    T["t_sbm"], T["t_slm"], T["t_wnm"], T["t_cpm"] = bf(sbm), bf(slm), bf(wnm), bf(cpm)
    T["t_csh"] = bf(csh)
    T["t_frc"] = frc
    OFFS = max(8 * (NSLOT - 1), 4)
    NREL = OFFS + 8
    bc = np.zeros((128, 8, NREL), np.float32)
    for h in range(8):
        for r in range(NREL):
            bc[:, h, r] = SLOPES[h] * (128 * (r - OFFS) + np.arange(128))
    T["t_bc"] = bc
    bcc = np.zeros((64, 8, NSLOT), np.float32)
    for h in range(8):
        for dlt in range(NSLOT):
            bcc[:, h, dlt] = SLOPES[h] * (16 * np.arange(64) + 15 - 1024 * dlt)
    T["t_bcc"] = bcc
    selh = np.zeros((8, 8, 128), np.float32)
    for h in range(8):
        selh[h, h, :] = 1.0
    T["t_selh"] = bf(selh)
    selg = np.zeros((24, 24, 128), np.float32)
    for k in range(24):
        selg[k, k, :] = 1.0
    T["t_selg"] = bf(selg)
    ne = np.zeros((65, NKB, 128), np.float32)
    ne[64] = 1.0
    for kb in range(NKB):
        for s in range(128):
            ne[(kb * 128 + s) // 64, kb, s] = NEG
    T["t_negexp"] = bf(ne)
    ms = np.zeros((64, NSLOT, 64), np.float32)
    wts = {0: 1.0, 1: 2.0, 2: 2.0, 3: 2.0, 4: 1.0}
    for g in range(NSLOT):
        for pp in range(64):
            n = 64 * g - 1 + pp
            if n < 0 or n >= n_cmp:
                continue
            for j in range(64):
                o = n - 4 * j
                if o in wts:
                    ms[pp, g, j] = wts[o]
    T["t_mslc"] = bf(ms)
    T["t_ident"] = bf(np.eye(128))
    T["t_ones"] = bf(np.ones((128, 128)))
    T["t_negones"] = bf(-np.ones((128, 128)))
    T["t_onesf"] = np.ones((128, 128), np.float32)
    jj = np.arange(128)[:, None]
    ss = np.arange(128)[None, :]
    T["t_negtri"] = bf(np.where(jj >= ss, -1.0, 0.0))
    return T


TABLE_DT = {"t_sel": F32, "t_frc": F32, "t_bc": F32, "t_bcc": F32, "t_onesf": F32}


def build_program(S, dbg=False):
    NSLOT = S // 1024
    NKB = S // 128
    OFFS = max(8 * (NSLOT - 1), 4)
    NREL = OFFS + 8
    nc = bass.Bass("TRN2", target_bir_lowering=False)

    def din(name, shape, dt=F32):
        return nc.dram_tensor(name, list(shape), dt, kind="ExternalInput").ap()

    x = din("x", [S, D])
    gains = din("gains", [6, D])
    w1i, w1o = din("ffn1_w_in", [D, 2 * DFF]), din("ffn1_w_out", [DFF, D])
    w2i, w2o = din("ffn2_w_in", [D, 2 * DFF]), din("ffn2_w_out", [DFF, D])
    win = din("w_in", [D, INCOLS])
    cw1 = [din("cmp_k_w1", [4096, 128]), din("cmp_v_w1", [4096, 128])]
    cw2 = [din("cmp_k_w2", [128, 128]), din("cmp_v_w2", [128, 128])]
    cpos = [din("cmp_pos_k", [32, 128]), din("cmp_pos_v", [32, 128])]
    wbsb, wbnsa, wout = din("w_branch_sb", [1024, D]), din("w_branch_nsa", [1024, D]), din("w_out", [D, D])
    tsh = make_tables(S, 0)
    tin = {k: din(k, v.shape, TABLE_DT.get(k, BF16)) for k, v in tsh.items()}
    out = nc.dram_tensor("out", [NSLOT * 512, D], F32, kind="ExternalOutput").ap()
    dbg_out = {}
    if dbg:
        dbg_out["d_x1"] = nc.dram_tensor("d_x1", [NSLOT * 512, D], F32, kind="ExternalOutput").ap()
        dbg_out["d_sb"] = nc.dram_tensor("d_sb", [NSLOT, 8, 128, 512], BF16, kind="ExternalOutput").ap()
        dbg_out["d_nsa"] = nc.dram_tensor("d_nsa", [NSLOT, 8, 128, 512], BF16, kind="ExternalOutput").ap()
        dbg_out["d_x2"] = nc.dram_tensor("d_x2", [NSLOT * 512, D], F32, kind="ExternalOutput").ap()
        dbg_out["d_hT"] = nc.dram_tensor("d_hT", [128, 16, 512], BF16, kind="ExternalOutput").ap()
        dbg_out["d_aT"] = nc.dram_tensor("d_aT", [128, NFF, 512], BF16, kind="ExternalOutput").ap()
        dbg_out["d_xs"] = nc.dram_tensor("d_xs", [4, 128, D], F32, kind="ExternalOutput").ap()
        dbg_out["d_y"] = nc.dram_tensor("d_y", [4, 128, D], F32, kind="ExternalOutput").ap()
        dbg_out["d_hbf"] = nc.dram_tensor("d_hbf", [128, 4, D], BF16, kind="ExternalOutput").ap()
        dbg_out["d_id"] = nc.dram_tensor("d_id", [128, 128], BF16, kind="ExternalOutput").ap()
        dbg_out["d_gcol"] = nc.dram_tensor("d_gcol", [128, 48], F32, kind="ExternalOutput").ap()
        dbg_out["d_stat"] = nc.dram_tensor("d_stat", [4, 128, 16], F32, kind="ExternalOutput").ap()

    def dscr(name, shape):
        return nc.dram_tensor(name, list(shape), BF16, kind="Internal").ap()

    KT_SB = dscr("kt_sb", [8, 128, S])
    V_SB = dscr("v_sb", [8, 128, NKB, 128])
    KT_CMP = dscr("kt_cmp", [2, 2, 128, S])
    KT_SLC, KT_WIN = dscr("kt_slc", [2, 128, S]), dscr("kt_win", [2, 128, S])
    V_SLC, V_WIN = dscr("v_slc", [2, 128, NKB, 128]), dscr("v_win", [2, 128, NKB, 128])

    P = Prog(nc)
    I = P.issue

    def sb(name, cols, dt):
        return nc.alloc_sbuf_tensor(name, [128, cols], dt)

    X1OWN = sb("x1own", 4 * D, F32)
    WST = [sb("wst%d" % i, 4096, BF16) for i in range(3)]
    KC = sb("kc", 2 * NSLOT * 64, BF16)
    VC = sb("vc", 2 * NSLOT * 128, BF16)
    CW2 = sb("cw2", 256, BF16)
    POSB = sb("posb", 2, F32)
    GCOL = sb("gcol", 3 * 16, F32)
    STAT = sb("stat", 32, F32)
    tb = {}
    for k, v in tsh.items():
        cols = int(np.prod(v.shape[1:]))
        tb[k] = None
    REGION_COLS = 62464
    REGION = sb("region", REGION_COLS, BF16)
    PSALL = nc.alloc_psum_tensor("psall", [128, 4096], F32)
    PSB = [PSALL[:, b * 512:(b + 1) * 512] for b in range(8)]
    PSBF = PSALL[:, 7 * 512:8 * 512].bitcast(BF16)
    psk = ["ps%d" % b for b in range(8)]

    static_tabs = ["t_sel", "t_bc", "t_bcc", "t_selh", "t_selg", "t_negexp", "t_mslc", "t_ident", "t_ones",
                   "t_negones", "t_negtri", "t_onesf"]
    for k in static_tabs:
        v = tsh[k]
        cols = int(np.prod(v.shape[1:]))
        dt = TABLE_DT.get(k, BF16)
        t_ = nc.alloc_sbuf_tensor("s_" + k, [128, cols], dt)
        tb[k] = t_
        npart = v.shape[0]
        src = tin[k]
        if len(v.shape) == 3:
            src = src.rearrange("p a b -> p (a b)")
        I("pool", lambda e, t_=t_, npart=npart, src=src: e.dma_start(out=t_[0:npart, :], in_=src), writes=[k], dma=True)
    IDENT = tb["t_ident"]
    ONESB = tb["t_ones"]
    NEGONES = tb["t_negones"]
    NEGTRI = tb["t_negtri"]
    SELT = tb["t_sel"]
    BC = tb["t_bc"]
    BCC = tb["t_bcc"]
    SELH = tb["t_selh"]
    SELG = tb["t_selg"]
    NEGEXP = tb["t_negexp"]
    MSLC = tb["t_mslc"]
    ONESF = tb["t_onesf"]

    for gi, row in enumerate([0, 2, 4]):
        I("pool", lambda e, gi=gi, row=row: e.dma_start(out=GCOL[:, gi * 16:(gi + 1) * 16],
                                                     in_=gains[row, :].rearrange("(c p) -> p c", p=128),
                                                     allow_slow_non_contiguous=True),
          writes=["gcol"], dma=True)
    I("pool", lambda e: e.dma_start(out=CW2[:, 0:128], in_=cw2[0]), writes=["cw2"], dma=True)
    I("pool", lambda e: e.dma_start(out=CW2[:, 128:256], in_=cw2[1]), writes=["cw2"], dma=True)

    roff = [0]

    def rreset(o=0):
        roff[0] = o

    def carve(cols, dt):
        n = cols * 2 if dt == F32 else cols
        a = REGION[:, roff[0]:roff[0] + n]
        roff[0] += n
        assert roff[0] <= REGION_COLS, roff[0]
        return a.bitcast(F32) if dt == F32 else a

    wrr = [0]
    NPACK = 256

    packed = {}
    pack_specs = []
    PACK = nc.dram_tensor("wpack", [NPACK, 128, 4096], BF16, kind="Internal").ap()

    def register(jid, parts):
        assert jid not in packed, jid
        packed[jid] = len(pack_specs)
        assert len(pack_specs) < NPACK
        pack_specs.append([jid, parts, False])

    def part_views(base, parts):
        outs = []
        o = 0
        for (src_ap, shape) in parts:
            n = int(np.prod(shape[1:]))
            v = base[:, o:o + n]
            if len(shape) == 3:
                v = v.rearrange("p (a b) -> p a b", a=shape[1])
            outs.append(v)
            o += n
        return outs, o

    def emit_pack(idx):
        jid, parts, emitted = pack_specs[idx]
        if emitted:
            return
        pack_specs[idx][2] = True
        views, _ = part_views(PACK[idx], parts)
        for pi, ((src_ap, shape), v) in enumerate(zip(parts, views)):
            I("pool", lambda e: e.dma_start(out=v, in_=src_ap), writes=["pack%d_%d" % (idx, pi)], dma=True, dgroup="pk")

    pack_next = [0]

    def pack_more(n):
        while n > 0 and pack_next[0] < len(pack_specs):
            if not pack_specs[pack_next[0]][2]:
                emit_pack(pack_next[0])
                n -= 1
            pack_next[0] += 1

    def stage2(srcs, shapes, jid):
        i = wrr[0]
        wrr[0] = (i + 1) % 3
        keys = ["wst%da" % i, "wst%db" % i]
        idx = packed[jid]
        emit_pack(idx)
        parts = pack_specs[idx][1]
        views, tot = part_views(WST[i], parts)
        I("sp", lambda e: e.dma_start(out=WST[i][:, 0:tot], in_=PACK[idx][:, 0:tot]),
          reads=["pack%d_%d" % (idx, pi) for pi in range(len(parts))], writes=keys, dma=True)
        return [(v, keys) for v in views]

    def stage(src_ap, shape, jid):
        (v, keys), = stage2([src_ap], [shape], jid)
        return v, keys

    def pipeline(jobs, depth=2):
        handles = {}
        n = len(jobs)
        for j in range(min(depth, n)):
            handles[j] = jobs[j][0]()
        for j in range(n):
            jobs[j][1](handles.pop(j))
            if j + depth < n:
                handles[j + depth] = jobs[j + depth][0]()

    def wview(w, p=128):
        return w.rearrange("(k p) c -> p k c", p=p)

    def register_all():
        def reg_ffn(w_in_d, w_out_d):
            wvi = wview(w_in_d)
            for c in range(NFF):
                register((w_in_d.tensor.name, "in", c), [(wvi[:, :, c * 128:(c + 1) * 128], [128, 16, 128]),
                                                        (wvi[:, :, DFF + c * 128:DFF + (c + 1) * 128], [128, 16, 128])])
            wvo = wview(w_out_d)
            for k0 in range(0, NFF, 2):
                register((w_out_d.tensor.name, "out", k0), [(wvo[:, k0:k0 + 2, :], [128, 2, 2048])])
        reg_ffn(w1i, w1o)
        wv = wview(win)
        kvcols = [C_KSB + 128 * h for h in range(8)] + [C_KCMP + 128 * g for g in range(2)] + [C_VCMP + 128 * g for g in range(2)] \
            + [C_KSLC + 128 * g for g in range(2)] + [C_KWIN + 128 * g for g in range(2)]
        for col0 in kvcols:
            register(("win", col0), [(wv[:, :, col0:col0 + 128], [128, 16, 128])])
        for c0 in (C_VSB, C_VSB + 512):
            for kh in range(2):
                register(("winv", c0, kh), [(wv[:, kh * 8:(kh + 1) * 8, c0:c0 + 512], [128, 8, 512])])
        for kh in range(2):
            register(("winv2", kh), [(wv[:, kh * 8:(kh + 1) * 8, C_VSLC:C_VSLC + 256], [128, 8, 256]),
                                     (wv[:, kh * 8:(kh + 1) * 8, C_VWIN:C_VWIN + 256], [128, 8, 256])])
        for h in range(16):
            col0 = (C_QSB + 128 * h) if h < 8 else (C_QNSA + 128 * (h - 8))
            register(("win", col0), [(wv[:, :, col0:col0 + 128], [128, 16, 128])])
        register(("win", C_GATE), [(wv[:, :, C_GATE:C_GATE + 24], [128, 16, 24])])
        for kv in range(2):
            register(("cw1", kv), [(cw1[kv].rearrange("(l d) o -> d l o", d=128), [128, 32, 128])])
        for c in range(16):
            register(("wb", c), [(wbsb.rearrange("(h p) c -> p h c", p=128)[:, :, c * 128:(c + 1) * 128], [128, 8, 128]),
                                 (wbnsa.rearrange("(h p) c -> p h c", p=128)[:, :, c * 128:(c + 1) * 128], [128, 8, 128])])
            register(("wm", c), [(wv[:, :, C_MERGE + c * 128:C_MERGE + (c + 1) * 128], [128, 16, 128]),
                                 (wv[:, :, C_MERGE + 2048 + c * 128:C_MERGE + 2048 + (c + 1) * 128], [128, 16, 128])])
        wvo = wview(wout)
        for k0 in range(0, 16, 2):
            register((wout.tensor.name, "out", k0), [(wvo[:, k0:k0 + 2, :], [128, 2, 2048])])
        reg_ffn(w2i, w2o)

    register_all()

    stat_i = [0]

    def stats_begin(src_ap, src_keys, junk_ap, junk_key):
        j = stat_i[0] % 8
        stat_i[0] += 1
        ss = STAT[:, 4 * j:4 * j + 1]
        rs = STAT[:, 4 * j + 1:4 * j + 2]
        k = "stat%d" % j
        I("dve", lambda e: e.memset(ss, 0.0), writes=[k])
        I("act", lambda e: e.activation(out=junk_ap, in_=src_ap, func=AF.Square, accum_out=ss),
          reads=list(src_keys) + [k], writes=[junk_key, k])
        return ss, rs, k

    def stats_finish(ss, rs, k, coef=1.0):
        I("dve", lambda e: e.tensor_scalar(out=rs, in0=ss, scalar1=1.0 / (D * coef * coef), scalar2=1e-6 / (coef * coef), op0=ALU.mult, op1=ALU.add),
          reads=[k], writes=[k])
        I("dve", lambda e: e.reciprocal(out=rs, in_=rs), reads=[k], writes=[k])
        I("act", lambda e: e.activation(out=rs, in_=rs, func=AF.Sqrt), reads=[k], writes=[k])

    def rms_stats(src_ap, src_keys, junk_ap, junk_key):
        ss, rs, k = stats_begin(src_ap, src_keys, junk_ap, junk_key)
        stats_finish(ss, rs, k)
        return rs, k

    def transposes_to_hT(hbf_ap, hbf_key, hT, hT_key, ts, gi):
        for half in range(2):
            for c8 in range(8):
                cc = half * 8 + c8
                I("pe", lambda e: e.transpose(out=PSBF[:, c8 * 128:(c8 + 1) * 128], in_=hbf_ap[:, cc * 128:(cc + 1) * 128], identity=IDENT[:, :]),
                  reads=[hbf_key, "t_ident"], writes=[psk[7]])
            gsl = GCOL[:, gi * 16 + half * 8: gi * 16 + half * 8 + 8].unsqueeze(2).to_broadcast([128, 8, 128])
            I("dve", lambda e: e.tensor_tensor(out=hT[:, half * 8:(half + 1) * 8, ts * 128:(ts + 1) * 128],
                                               in0=PSBF.rearrange("p (k t) -> p k t", k=8), in1=gsl, op=ALU.mult),
              reads=[psk[7], "gcol"], writes=[hT_key])

    def prenorm(src_ap, src_keys, junk_ap, junk_key, hbf_ap, hbf_key):
        rs, k = rms_stats(src_ap, src_keys, junk_ap, junk_key)
        I("dve", lambda e: e.tensor_scalar(out=hbf_ap, in0=src_ap, scalar1=rs, scalar2=1.0, op0=ALU.mult, op1=ALU.mult),
          reads=list(src_keys) + [k], writes=[hbf_key])

    def ffn_in(w_in_d, hT, hT_key, aT, sg):
        wv = wview(w_in_d)
        jobs = []
        for c in range(NFF):
            def load(c=c):
                return stage2([wv[:, :, c * 128:(c + 1) * 128], wv[:, :, DFF + c * 128:DFF + (c + 1) * 128]],
                              [[128, 16, 128], [128, 16, 128]], (w_in_d.tensor.name, "in", c))

            def comp(h, c=c):
                pack_more(2)
                b0 = (c % 2) * 2
                for j, (wt, wkeys) in enumerate(h):
                    for k in range(16):
                        I("pe", lambda e, wt=wt, k=k, b=b0 + j: e.matmul(PSB[b], lhsT=wt[:, k, :], rhs=hT[:, k, :],
                                                                         start=(k == 0), stop=(k == 15)),
                          reads=wkeys + [hT_key], writes=[psk[b0 + j]])
                s_ = sg[c % 2]
                sk_ = "sg%d" % (c % 2)
                I("act", lambda e, s_=s_, b=b0: e.activation(out=s_, in_=PSB[b], func=AF.Silu), reads=[psk[b0]], writes=[sk_])
                I("dve", lambda e, s_=s_, b=b0 + 1, c=c: e.tensor_tensor(out=aT[:, c, :], in0=s_, in1=PSB[b], op=ALU.mult),
                  reads=[sk_, psk[b0 + 1]], writes=["aT%d" % c])
            jobs.append((load, comp))
        pipeline(jobs, 2)

    def tok_out(lhs_fn, lhs_keys_fn, nk, w_d, epilogue, GB, TMPS, coef, pre_pair=None):
        wv = wview(w_d)
        for pair in range(2):
            if pre_pair is not None:
                pre_pair(pair)
            jobs = []
            for k0 in range(0, nk, 2):
                def load(k0=k0):
                    return stage(wv[:, k0:k0 + 2, :], [128, 2, 2048], (w_d.tensor.name, "out", k0))

                def comp(h, k0=k0, pair=pair):
                    pack_more(2)
                    wt, key = h
                    for kk in range(2):
                        k = k0 + kk
                        for tl in range(2):
                            ts = pair * 2 + tl
                            for cp in range(4):
                                b = tl * 4 + cp
                                I("pe", lambda e, k=k, kk=kk, ts=ts, cp=cp, b=b: e.matmul(
                                    PSB[b], lhsT=lhs_fn(k)[:, ts * 128:(ts + 1) * 128],
                                    rhs=wt[:, kk, cp * 512:(cp + 1) * 512], start=(k == 0), stop=(k == nk - 1)),
                                  reads=key + lhs_keys_fn(k), writes=[psk[b]])
                jobs.append((load, comp))
            pipeline(jobs, 2)
            st = []
            for tl in range(2):
                ps_ap = PSALL[:, tl * 2048:(tl + 1) * 2048]
                ps_keys = [psk[tl * 4 + cp] for cp in range(4)]
                tm, tk = TMPS[tl], "tmp%d" % tl
                ss, rs, k = stats_begin(ps_ap, ps_keys, tm, tk)
                I("dve", lambda e: e.tensor_tensor(out=tm, in0=ps_ap, in1=GB, op=ALU.mult), reads=ps_keys + ["GB"], writes=[tk])
                st.append((ss, rs, k, tm, tk))
            for tl in range(2):
                ss, rs, k, tm, tk = st[tl]
                stats_finish(ss, rs, k, coef)
                epilogue(pair * 2 + tl, rs, k, tm, tk)

    def resid_update(xres_ap, xres_keys, rs, k, tm, tk):
        I("dve", lambda e: e.scalar_tensor_tensor(out=xres_ap, in0=tm, scalar=rs, in1=xres_ap, op0=ALU.mult, op1=ALU.add),
          reads=[tk, k] + list(xres_keys), writes=list(xres_keys))

    def load_GB(GB, row):
        I("pool", lambda e: e.dma_start(out=GB, in_=gains[row, :].partition_broadcast(128)), writes=["GB"], dma=True)

    def fm_proj_job(col0, ncols, hT, hT_key, bank, consume):
        wv = wview(win)

        def load():
            return stage(wv[:, :, col0:col0 + ncols], [128, 16, ncols], ("win", col0))

        def comp(h):
            pack_more(1)
            wt, key = h
            for k in range(16):
                I("pe", lambda e, k=k: e.matmul(PSB[bank][0:ncols, :], lhsT=wt[:, k, :], rhs=hT[:, k, :],
                                                start=(k == 0), stop=(k == 15)),
                  reads=key + [hT_key], writes=[psk[bank]])
            consume(PSB[bank][0:ncols, :], psk[bank])
        return (load, comp)

    for i in range(NSLOT):
        Nk = 1024 * (i + 1)
        nkb = 8 * (i + 1)
        for j in range(2):
            if j == 0:
                P.fence()
            T0 = 1024 * i + 512 * j
            rreset()
            hT = carve(16 * 512, BF16).rearrange("p (k t) -> p k t", k=16)
            aT = carve(NFF * 512, BF16).rearrange("p (k t) -> p k t", k=NFF)
            GB = carve(D, F32)
            xs = carve(D, F32)
            tmp = carve(D, F32)
            tmp2 = carve(D, F32)
            hbf4 = carve(4 * D, BF16).rearrange("p (a b) -> p a b", a=4)
            sg = [carve(512, F32), carve(512, F32)]
            kvst = [carve(512, BF16), carve(512, BF16)]
            vout = [carve(512, BF16), carve(512, BF16)]
            xbufs = [(xs, "xs"), (tmp2, "tmp1"), (GB, "GB")]
            def xload(ts):
                xb, xk = xbufs[ts % 3]
                I("pool", lambda e: e.dma_start(out=xb, in_=x[T0 + ts * 128:T0 + (ts + 1) * 128, :]), writes=[xk], dma=True)
            for ts in range(3):
                xload(ts)
            for ts in range(4):
                xb, xk = xbufs[ts % 3]
                prenorm(xb, [xk], tmp, "tmp0", hbf4[:, ts, :], "hbf%d" % ts)
                if ts == 0:
                    xload(3)
                transposes_to_hT(hbf4[:, ts, :], "hbf%d" % ts, hT, "hT", ts, 0)
            if dbg and i == 0 and j == 0:
                I("pool", lambda e: e.dma_start(out=dbg_out["d_hT"], in_=hT), reads=["hT"], writes=["dbg"], dma=True)
            ffn_in(w1i, hT, "hT", aT, sg)
            if dbg and i == 0 and j == 0:
                I("pool", lambda e: e.dma_start(out=dbg_out["d_aT"], in_=aT), reads=["aT%d" % c for c in range(NFF)], writes=["dbg"], dma=True)
            load_GB(GB, 1)

            xs2 = hT.rearrange("p k t -> p (k t)")[:, 0:2 * D].bitcast(F32)
            xeb = [(xs, "xs"), (xs2, "hT")]

            def pre_a(pair, T0=T0):
                for tl in range(2):
                    ts = pair * 2 + tl
                    xb, xk = xeb[tl]
                    I("pool", lambda e: e.dma_start(out=xb, in_=x[T0 + ts * 128:T0 + (ts + 1) * 128, :]), writes=[xk], dma=True)

            def epi_a(ts, rs, k, tm, tk, T0=T0, j=j, i=i):
                xb, xk = xeb[ts % 2]
                resid_update(xb, [xk], rs, k, tm, tk)
                sc = SELT[:, 2 * i + j:2 * i + j + 1]
                if j == 0:
                    I("dve", lambda e: e.tensor_scalar(out=X1OWN[:, ts * D:(ts + 1) * D], in0=xb, scalar1=sc, scalar2=1.0,
                                                       op0=ALU.mult, op1=ALU.mult), reads=[xk, "t_sel"], writes=["x1own%d" % ts])
                else:
                    I("dve", lambda e: e.scalar_tensor_tensor(out=X1OWN[:, ts * D:(ts + 1) * D], in0=xb, scalar=sc,
                                                              in1=X1OWN[:, ts * D:(ts + 1) * D], op0=ALU.mult, op1=ALU.add),
                      reads=[xk, "t_sel", "x1own%d" % ts], writes=["x1own%d" % ts])
                prenorm(xb, [xk], tm, tk, hbf4[:, ts, :], "hbf%d" % ts)

            tok_out(lambda k: aT[:, k, :], lambda k: ["aT%d" % k], NFF, w1o, epi_a, GB, [tmp, tmp2], 0.5, pre_pair=pre_a)
            for ts in range(4):
                transposes_to_hT(hbf4[:, ts, :], "hbf%d" % ts, hT, "hT", ts, 1)
            jobs = []
            fm = [(C_KSB + 128 * h, KT_SB[h]) for h in range(8)]
            fm += [(C_KCMP + 128 * g, KT_CMP[0, g]) for g in range(2)] + [(C_VCMP + 128 * g, KT_CMP[1, g]) for g in range(2)]
            fm += [(C_KSLC + 128 * g, KT_SLC[g]) for g in range(2)] + [(C_KWIN + 128 * g, KT_WIN[g]) for g in range(2)]
            for n_, (col0, dst) in enumerate(fm):
                def consume(ps_ap, pskey, n_=n_, dst=dst, T0=T0):
                    st = kvst[n_ % 2]
                    sk_ = "kvst%d" % (n_ % 2)
                    I("act", lambda e: e.copy(out=st, in_=ps_ap), reads=[pskey], writes=[sk_])
                    I("pool", lambda e: e.dma_start(out=dst[:, T0:T0 + 512], in_=st), reads=[sk_], dma=True)
                jobs.append(fm_proj_job(col0, 128, hT, "hT", n_ % 2, consume))
            pipeline(jobs, 2)
            wv = wview(win)
            BANKS_A = [2, 3, 4, 5]
            BANKS_B = [6, 7, 0, 1]
            panels = [("sb", [(C_VSB, 512)], 0, [BANKS_A]), ("sb", [(C_VSB + 512, 512)], 1, [BANKS_B]),
                      ("nsa", [(C_VSLC, 256), (C_VWIN, 256)], 0, [BANKS_A, BANKS_B])]
            jobs = []
            for kind, cols, pidx, bsets in panels:
                for kh in range(2):
                    def load(kind=kind, cols=cols, kh=kh):
                        if kind == "sb":
                            c0, w_ = cols[0]
                            wt, key = stage(wv[:, kh * 8:(kh + 1) * 8, c0:c0 + w_], [128, 8, w_], ("winv", c0, kh))
                            return [(wt, key, w_)]
                        outs_ = stage2([wv[:, kh * 8:(kh + 1) * 8, c0:c0 + w_] for (c0, w_) in cols],
                                       [[128, 8, w_] for (c0, w_) in cols], ("winv2", kh))
                        return [(wt, key, cols[n_][1]) for n_, (wt, key) in enumerate(outs_)]

                    def comp(wts, kind=kind, pidx=pidx, kh=kh, bsets=bsets, T0=T0):
                        o = 0
                        for wi, (wt, key, w_) in enumerate(wts):
                            for ts in range(4):
                                bk = bsets[wi][ts]
                                for kk in range(8):
                                    k = kh * 8 + kk
                                    I("pe", lambda e: e.matmul(PSB[bk][:, o:o + w_], lhsT=hT[:, k, ts * 128:(ts + 1) * 128], rhs=wt[:, kk, :],
                                                               start=(k == 0), stop=(k == 15)), reads=key + ["hT"], writes=[psk[bk]])
                            o += w_
                        if kh == 0:
                            return
                        for ts in range(4):
                            vo = vout[ts % 2]
                            vk = "vout%d" % (ts % 2)
                            kb = (T0 // 128) + ts
                            if kind == "sb":
                                bk = bsets[0][ts]
                                I("act", lambda e: e.copy(out=vo, in_=PSB[bk]), reads=[psk[bk]], writes=[vk])
                                dst = V_SB[4 * pidx:4 * pidx + 4, :, kb, :].rearrange("h p d -> p h d")
                                I("pool", lambda e: e.dma_start(out=dst, in_=vo.rearrange("p (h d) -> p h d", h=4)), reads=[vk], dma=True)
                            else:
                                b0_, b1_ = bsets[0][ts], bsets[1][ts]
                                I("act", lambda e: e.copy(out=vo[:, 0:256], in_=PSB[b0_][:, 0:256]), reads=[psk[b0_]], writes=[vk])
                                I("act", lambda e: e.copy(out=vo[:, 256:512], in_=PSB[b1_][:, 256:512]), reads=[psk[b1_]], writes=[vk])
                                d1 = V_SLC[:, :, kb, :].rearrange("g p d -> p g d")
                                d2 = V_WIN[:, :, kb, :].rearrange("g p d -> p g d")
                                I("pool", lambda e: e.dma_start(out=d1, in_=vo[:, 0:256].rearrange("p (g d) -> p g d", g=2)), reads=[vk], dma=True)
                                I("pool", lambda e: e.dma_start(out=d2, in_=vo[:, 256:512].rearrange("p (g d) -> p g d", g=2)), reads=[vk], dma=True)
                    jobs.append((load, comp))
            pipeline(jobs, 2)

        P.fence()
        rreset()
        H2 = carve(16 * 512, BF16).rearrange("p (k t) -> p k t", k=16)
        OUTS = carve(16 * 512, BF16).rearrange("p (h t) -> p h t", h=16)
        PREFIX = roff[0]
        QT = carve(16 * 512, BF16).rearrange("p (h t) -> p h t", h=16)
        GSIG = carve(512, BF16)
        ATT0 = roff[0]
        tmp = carve(D, F32)
        hbf4 = carve(4 * D, BF16).rearrange("p (a b) -> p a b", a=4)
        if dbg:
            for ts in range(4):
                I("pool", lambda e, ts=ts: e.dma_start(out=dbg_out["d_x1"][i * 512 + ts * 128:i * 512 + (ts + 1) * 128, :],
                                                     in_=X1OWN[:, ts * D:(ts + 1) * D]), reads=["x1own%d" % ts], writes=["dbg"], dma=True)
        for ts in range(4):
            prenorm(X1OWN[:, ts * D:(ts + 1) * D], ["x1own%d" % ts], tmp, "tmp0", hbf4[:, ts, :], "hbf%d" % ts)
            transposes_to_hT(hbf4[:, ts, :], "hbf%d" % ts, H2, "H2", ts, 1)
        jobs = []
        for h in range(16):
            col0 = (C_QSB + 128 * h) if h < 8 else (C_QNSA + 128 * (h - 8))

            def consume(ps_ap, pskey, h=h):
                I("act", lambda e: e.activation(out=QT[:, h, :], in_=ps_ap, func=AF.Copy, scale=float(HD ** -0.5)),
                  reads=[pskey], writes=["qt%d" % h])
            jobs.append(fm_proj_job(col0, 128, H2, "H2", h % 2, consume))

        def consume_gate(ps_ap, pskey):
            eg = tmp[0:24, 0:512]
            I("act", lambda e: e.activation(out=eg, in_=ps_ap, func=AF.Exp, scale=-1.0), reads=[pskey], writes=["tmp0"])
            I("dve", lambda e: e.tensor_scalar(out=eg, in0=eg, scalar1=1.0, scalar2=1.0, op0=ALU.add, op1=ALU.mult),
              reads=["tmp0"], writes=["tmp0"])
            I("dve", lambda e: e.reciprocal(out=GSIG[0:24, :], in_=eg), reads=["tmp0"], writes=["gsig"])
        jobs.append(fm_proj_job(C_GATE, 24, H2, "H2", 2, consume_gate))
        pipeline(jobs, 2)

        P.fence()
        rreset(ATT0)
        KTB = [carve(S, BF16), carve(1536, BF16)]
        VB = [carve(NKB * 128, BF16).rearrange("p (k d) -> p k d", d=128), carve(1536, BF16).rearrange("p (k d) -> p k d", d=128)]
        MSK = carve(12 * 512, BF16).rearrange("p (m t) -> p m t", t=512)
        CSH = carve(512, BF16)
        E1 = [carve(512, F32), carve(512, F32)]
        LB = [carve(512, BF16), carve(512, BF16)]
        LSUM = carve(512, F32)
        LHI = carve(512, BF16)
        LLO = carve(512, BF16)
        WB = [carve(512, BF16), carve(512, BF16)]
        RR = carve(512, F32)
        RG = carve(512, F32)
        ACC = [carve(512, F32) for _ in range(4)]
        PB = [carve(512, BF16) for _ in range(2)]
        PGF = [E1[0], E1[1], LSUM, carve(512, F32)]
        PGK = ["e10", "e11", "lsum", "pgf3"]
        HIS = [(LHI, "lhi"), (PB[0], "pb0")]
        LOS = [(LLO, "llo"), (PB[1], "pb1")]
        SC = carve(256, F32)
        SC2 = carve(64, F32)
        M8 = carve(16, F32)
        NSEL = carve(64, BF16)
        NSB = [carve(512, BF16), carve(512, BF16)]
        FRC = carve(256, F32)
        CBUF = carve(1040, BF16)
        CX = [carve(64, F32) for _ in range(3)]
        GL = carve(64, BF16)

        I("pool", lambda e: e.dma_start(out=CSH[0:8, :], in_=tin["t_csh"][i]), writes=["csh"], dma=True)
        I("pool", lambda e: e.dma_start(out=FRC, in_=tin["t_frc"][i].rearrange("p a b -> p (a b)")), writes=["frc"], dma=True)

        I("pool", lambda e: e.dma_start(out=MSK[:, 0:8, :], in_=tin["t_sbm"][i]), writes=["msk"], dma=True)
        for h in range(8):
            kt, vb = KTB[0], VB[0]
            hk = nkb // 2
            I("pool", lambda e: e.dma_start(out=kt[:, hk * 128:Nk], in_=KT_SB[h][:, hk * 128:Nk]), writes=["ktb0h"], dma=True)
            I("pool", lambda e: e.dma_start(out=vb[:, hk:nkb, :], in_=V_SB[h][:, hk:nkb, :]), writes=["vb0h"], dma=True)
            I("pool", lambda e: e.dma_start(out=kt[:, 0:hk * 128], in_=KT_SB[h][:, 0:hk * 128]), writes=["ktb0l"], dma=True)
            I("pool", lambda e: e.dma_start(out=vb[:, 0:hk, :], in_=V_SB[h][:, 0:hk, :]), writes=["vb0l"], dma=True)
            I("dve", lambda e: e.memset(LSUM, 0.0), writes=["lsum"])
            ob = 4 + (h % 2)
            kbs = list(range(nkb - 1, -1, -1))
            NS = len(kbs)

            def sbA(n):
                kb = kbs[n]
                diag = kb >= 8 * i
                m = kb - 8 * i
                b1, b2 = n % 2, 2 + (n % 2)
                ktk = "ktb0h" if kb >= hk else "ktb0l"
                I("pe", lambda e: e.matmul(PSB[b1], lhsT=kt[:, kb * 128:(kb + 1) * 128], rhs=QT[:, h, :], start=True, stop=(not diag)),
                  reads=[ktk, "qt%d" % h], writes=[psk[b1]])
                if diag:
                    I("pe", lambda e: e.matmul(PSB[b1], lhsT=IDENT[:, :], rhs=MSK[:, m, :], start=False, stop=True),
                      reads=["t_ident", "msk"], writes=[psk[b1]])
                I("pe", lambda e: e.matmul(PSB[b2], lhsT=kt[:, kb * 128:(kb + 1) * 128], rhs=QT[:, h, :], start=True, stop=False),
                  reads=[ktk, "qt%d" % h], writes=[psk[b2]])
                if diag:
                    I("pe", lambda e: e.matmul(PSB[b2], lhsT=IDENT[:, :], rhs=MSK[:, m, :], start=False, stop=False),
                      reads=["t_ident", "msk"], writes=[psk[b2]])

            def sbB(n):
                b1 = n % 2
                e1, lb = E1[n % 2], LB[n % 2]
                I("act", lambda e: e.activation(out=e1, in_=PSB[b1], func=AF.Exp), reads=[psk[b1]], writes=["e1%d" % (n % 2)])
                I("act", lambda e: e.activation(out=lb, in_=e1, func=AF.Ln, bias=1.0, scale=1.0), reads=["e1%d" % (n % 2)], writes=["lb%d" % (n % 2)])

            def sbC(n):
                b2 = 2 + (n % 2)
                lb = LB[n % 2]
                I("pe", lambda e: e.matmul(PSB[b2], lhsT=NEGTRI[:, :], rhs=lb, start=False, stop=(n == 0)),
                  reads=["t_negtri", "lb%d" % (n % 2)], writes=[psk[b2]])
                if n > 0:
                    hi_, hik = HIS[(n - 1) % 2]
                    lo_, lok = LOS[(n - 1) % 2]
                    I("pe", lambda e: e.matmul(PSB[b2], lhsT=NEGONES[:, :], rhs=hi_, start=False, stop=False),
                      reads=["t_negones", hik], writes=[psk[b2]])
                    I("pe", lambda e: e.matmul(PSB[b2], lhsT=NEGONES[:, :], rhs=lo_, start=False, stop=True),
                      reads=["t_negones", lok], writes=[psk[b2]])

            def sbD(n):
                b2 = 2 + (n % 2)
                lb, wb = LB[n % 2], WB[n % 2]
                I("act", lambda e: e.activation(out=wb, in_=PSB[b2], func=AF.Exp), reads=[psk[b2]], writes=["wb%d" % (n % 2)])
                I("dve", lambda e: e.tensor_tensor(out=LSUM, in0=LSUM, in1=lb, op=ALU.add), reads=["lsum", "lb%d" % (n % 2)], writes=["lsum"])
                if n + 1 < NS:
                    hi_, hik = HIS[n % 2]
                    lo_, lok = LOS[n % 2]
                    I("dve", lambda e: e.tensor_copy(out=hi_, in_=LSUM), reads=["lsum"], writes=[hik])
                    I("dve", lambda e: e.tensor_tensor(out=lo_, in0=LSUM, in1=hi_, op=ALU.subtract), reads=["lsum", hik], writes=[lok])

            def sbE(n):
                kb = kbs[n]
                wb = WB[n % 2]
                vbk = "vb0h" if kb >= hk else "vb0l"
                I("pe", lambda e: e.matmul(PSB[ob], lhsT=vb[:, kb, :], rhs=wb, start=(n == 0), stop=(n == NS - 1)),
                  reads=[vbk, "wb%d" % (n % 2)], writes=[psk[ob]])

            sbA(0)
            sbB(0)
            for n in range(NS):
                if n + 1 < NS:
                    sbA(n + 1)
                    sbB(n + 1)
                sbC(n)
                sbD(n)
                if n >= 1:
                    sbE(n - 1)
            sbE(NS - 1)
            I("dve", lambda e: e.tensor_copy(out=OUTS[:, h, :], in_=PSB[ob]), reads=[psk[ob]], writes=["outs%d" % h])

        for kv in range(2):
            w1t, w1key = stage(cw1[kv].rearrange("(l d) o -> d l o", d=128), [128, 32, 128], ("cw1", kv))
            if i == 0:
                pt = E1[0][:, 0:32]
                I("pool", lambda e, kv=kv, pt=pt: e.dma_start(out=pt, in_=cpos[kv].rearrange("l d -> d l"), allow_slow_non_contiguous=True),
                  writes=["e10"], dma=True)
                ptb = PB[0][:, 0:32]
                I("dve", lambda e, pt=pt, ptb=ptb: e.tensor_copy(out=ptb, in_=pt), reads=["e10"], writes=["pb0"])
                for l in range(32):
                    I("pe", lambda e, l=l, ptb=ptb, w1t=w1t: e.matmul(PSB[6][:, 0:1], lhsT=w1t[:, l, :], rhs=ptb[:, l:l + 1], start=(l == 0), stop=(l == 31)),
                      reads=w1key + ["pb0"], writes=[psk[6]])
                I("dve", lambda e, kv=kv: e.tensor_copy(out=POSB[:, kv:kv + 1], in_=PSB[6][:, 0:1]), reads=[psk[6]], writes=["posb"])
            for g in range(2):
                if i == 0:
                    I("dve", lambda e: e.memset(CBUF[:, 0:16], 0.0), writes=["cbuf"])
                    I("pool", lambda e, kv=kv, g=g: e.dma_start(out=CBUF[:, 16:1040], in_=KT_CMP[kv, g][:, 0:1024]), writes=["cbuf"], dma=True)
                else:
                    I("pool", lambda e, kv=kv, g=g: e.dma_start(out=CBUF[:, 0:1040], in_=KT_CMP[kv, g][:, 1024 * i - 16:1024 * i + 1024]),
                      writes=["cbuf"], dma=True)
                for l in range(32):
                    I("pe", lambda e, l=l, w1t=w1t: e.matmul(PSB[6][:, 0:64], lhsT=w1t[:, l, :], rhs=CBUF[:, l:l + 1009:16], start=(l == 0), stop=(l == 31)),
                      reads=w1key + ["cbuf"], writes=[psk[6]])
                xx, x2, uu = CX
                I("dve", lambda e, kv=kv: e.tensor_scalar(out=xx, in0=PSB[6][:, 0:64], scalar1=POSB[:, kv:kv + 1], scalar2=1.0, op0=ALU.add, op1=ALU.mult),
                  reads=[psk[6], "posb"], writes=["cx0"])
                I("dve", lambda e: e.tensor_tensor(out=x2, in0=xx, in1=xx, op=ALU.mult), reads=["cx0"], writes=["cx1"])
                I("dve", lambda e: e.tensor_scalar(out=x2, in0=x2, scalar1=0.044715, scalar2=1.0, op0=ALU.mult, op1=ALU.add), reads=["cx1"], writes=["cx1"])
                I("dve", lambda e: e.tensor_tensor(out=uu, in0=x2, in1=xx, op=ALU.mult), reads=["cx0", "cx1"], writes=["cx2"])
                I("act", lambda e: e.activation(out=uu, in_=uu, func=AF.Exp, scale=float(-2.0 * np.sqrt(2.0 / np.pi))), reads=["cx2"], writes=["cx2"])
                I("dve", lambda e: e.tensor_scalar(out=uu, in0=uu, scalar1=1.0, scalar2=1.0, op0=ALU.add, op1=ALU.mult), reads=["cx2"], writes=["cx2"])
                I("dve", lambda e: e.reciprocal(out=uu, in_=uu), reads=["cx2"], writes=["cx2"])
                I("dve", lambda e: e.tensor_tensor(out=GL, in0=xx, in1=uu, op=ALU.mult), reads=["cx0", "cx2"], writes=["gl"])
                if kv == 0:
                    I("pe", lambda e: e.matmul(PSB[6][:, 64:128], lhsT=CW2[:, 0:128], rhs=GL, start=True, stop=True), reads=["cw2", "gl"], writes=[psk[6]])
                    I("dve", lambda e, g=g: e.tensor_copy(out=KC[:, (g * NSLOT + i) * 64:(g * NSLOT + i + 1) * 64], in_=PSB[6][:, 64:128]),
                      reads=[psk[6]], writes=["kc"])
                else:
                    I("pe", lambda e: e.matmul(PSB[6][0:64, 128:256], lhsT=GL, rhs=CW2[:, 128:256], start=True, stop=True), reads=["cw2", "gl"], writes=[psk[6]])
                    I("dve", lambda e, g=g: e.tensor_copy(out=VC[0:64, (g * NSLOT + i) * 128:(g * NSLOT + i + 1) * 128], in_=PSB[6][0:64, 128:256]),
                      reads=[psk[6]], writes=["vc"])

        def gate_rg(branch, h, den_bank):
            I("dve", lambda e: e.tensor_scalar(out=RR, in0=PSB[den_bank], scalar1=1e-30, scalar2=1.0, op0=ALU.add, op1=ALU.mult),
              reads=[psk[den_bank]], writes=["rr"])
            I("dve", lambda e: e.reciprocal(out=RR, in_=RR), reads=["rr"], writes=["rr"])
            col = branch * 8 + h
            I("pe", lambda e: e.matmul(PSB[6], lhsT=SELG[0:24, col * 128:(col + 1) * 128], rhs=GSIG[0:24, :], start=True, stop=True),
              reads=["t_selg", "gsig"], writes=[psk[6]])
            I("dve", lambda e: e.tensor_tensor(out=RG, in0=RR, in1=PSB[6], op=ALU.mult), reads=["rr", psk[6]], writes=["rg"])

        def acc_branch(r, o_bank, firstb):
            if firstb:
                I("dve", lambda e: e.tensor_tensor(out=ACC[r], in0=PSB[o_bank], in1=RG, op=ALU.mult), reads=[psk[o_bank], "rg"], writes=["acc%d" % r])
            else:
                I("dve", lambda e: e.tensor_tensor(out=RG, in0=PSB[o_bank], in1=RG, op=ALU.mult), reads=[psk[o_bank], "rg"], writes=["rg"])
                I("dve", lambda e: e.tensor_tensor(out=ACC[r], in0=ACC[r], in1=RG, op=ALU.add), reads=["rg", "acc%d" % r], writes=["acc%d" % r])

        for g in range(2):
            I("pool", lambda e: e.dma_start(out=MSK[0:64, 0:NSLOT, :], in_=tin["t_cpm"][i]), writes=["msk"], dma=True)
            for r in range(4):
                h = 4 * g + r
                for gp in range(i + 1):
                    b = gp % 2
                    eb = WB[gp % 2] if gp < 2 else LB[gp % 2]
                    ebk = ("wb%d" if gp < 2 else "lb%d") % (gp % 2)
                    I("pe", lambda e, gp=gp, b=b, h=h: e.matmul(PSB[b][0:64, :], lhsT=KC[:, (g * NSLOT + gp) * 64:(g * NSLOT + gp + 1) * 64],
                                                             rhs=QT[:, 8 + h, :], start=True, stop=False),
                      reads=["kc", "qt%d" % (8 + h)], writes=[psk[b]])
                    I("pe", lambda e, b=b, h=h: e.matmul(PSB[b][0:64, :], lhsT=SELH[0:8, h * 128:h * 128 + 64], rhs=CSH[0:8, :], start=False, stop=False),
                      reads=["t_selh", "csh"], writes=[psk[b]])
                    I("pe", lambda e, b=b, gp=gp: e.matmul(PSB[b][0:64, :], lhsT=IDENT[0:64, 0:64], rhs=MSK[0:64, gp, :], start=False, stop=True),
                      reads=["t_ident", "msk"], writes=[psk[b]])
                    I("act", lambda e, b=b, gp=gp, h=h, eb=eb: e.activation(out=eb[0:64, :], in_=PSB[b][0:64, :], func=AF.Exp,
                                                                          bias=BCC[0:64, h * NSLOT + (i - gp):h * NSLOT + (i - gp) + 1], scale=1.0),
                      reads=[psk[b], "t_bcc"], writes=[ebk])
                    I("pe", lambda e, gp=gp, eb=eb: e.matmul(PSB[3], lhsT=ONESB[0:64, :], rhs=eb[0:64, :], start=(gp == 0), stop=(gp == i)),
                      reads=["t_ones", ebk], writes=[psk[3]])
                    I("pe", lambda e, gp=gp, eb=eb: e.matmul(PSB[2], lhsT=VC[0:64, (g * NSLOT + gp) * 128:(g * NSLOT + gp + 1) * 128], rhs=eb[0:64, :],
                                                          start=(gp == 0), stop=(gp == i)),
                      reads=["vc", ebk], writes=[psk[2]])
                gate_rg(0, h, 3)
                acc_branch(r, 2, True)
                for gp in range(i + 1):
                    eb = WB[gp % 2] if gp < 2 else LB[gp % 2]
                    ebk = ("wb%d" if gp < 2 else "lb%d") % (gp % 2)
                    pg, pgk = PGF[gp], PGK[gp]
                    if r == 0:
                        I("dve", lambda e, eb=eb, pg=pg: e.tensor_tensor(out=pg[0:64, :], in0=eb[0:64, :], in1=RR[0:64, :], op=ALU.mult),
                          reads=[ebk, "rr"], writes=[pgk])
                    else:
                        I("dve", lambda e, eb=eb: e.tensor_tensor(out=RG[0:64, :], in0=eb[0:64, :], in1=RR[0:64, :], op=ALU.mult),
                          reads=[ebk, "rr"], writes=["rg"])
                        I("dve", lambda e, pg=pg: e.tensor_tensor(out=pg[0:64, :], in0=pg[0:64, :], in1=RG[0:64, :], op=ALU.add),
                          reads=["rg", pgk], writes=[pgk])
            bgs = [(PB[0], "pb0"), (PB[1], "pb1"), (LB[0], "lb0"), (LB[1], "lb1")]
            for gp in range(i + 1):
                I("dve", lambda e, gp=gp: e.tensor_copy(out=bgs[gp][0][0:64, :], in_=PGF[gp][0:64, :]), reads=[PGK[gp]], writes=[bgs[gp][1]])
            for ts in range(4):
                for gp in range(i + 1):
                    I("pe", lambda e, ts=ts, gp=gp: e.matmul(PSB[5][:, ts * 64:(ts + 1) * 64], lhsT=bgs[gp][0][0:64, ts * 128:(ts + 1) * 128],
                                                         rhs=MSLC[0:64, gp * 64:(gp + 1) * 64], start=(gp == 0), stop=(gp == i)),
                      reads=[bgs[gp][1], "t_mslc"], writes=[psk[5]])
            I("dve", lambda e: e.tensor_tensor(out=SC, in0=PSB[5][:, 0:256], in1=FRC, op=ALU.add), reads=[psk[5], "frc"], writes=["sc"])
            for ts in range(4):
                scs = SC[:, ts * 64:(ts + 1) * 64]
                I("dve", lambda e, scs=scs: e.max(out=M8[:, 0:8], in_=scs), reads=["sc"], writes=["m8"])
                I("dve", lambda e, scs=scs: e.match_replace(out=SC2, in_to_replace=M8[:, 0:8], in_values=scs, imm_value=-3.0e38),
                  reads=["sc", "m8"], writes=["sc2"])
                I("dve", lambda e: e.max(out=M8[:, 8:16], in_=SC2), reads=["sc2"], writes=["m8"])
                I("dve", lambda e, scs=scs: e.tensor_scalar(out=NSEL, in0=scs, scalar1=M8[:, 15:16], scalar2=1.0, op0=ALU.is_lt, op1=ALU.mult),
                  reads=["sc", "m8"], writes=["nsel"])
                I("pe", lambda e, ts=ts: e.transpose(out=PSBF[0:64, ts * 128:(ts + 1) * 128], in_=NSEL, identity=IDENT[:, :]),
                  reads=["nsel", "t_ident"], writes=[psk[7]])
            I("dve", lambda e: e.tensor_copy(out=NSB[0][0:64, :], in_=PSBF[0:64, 0:512]), reads=[psk[7]], writes=["ns0"])
            I("dve", lambda e: e.tensor_copy(out=NSB[1][0:64, :], in_=PSBF[0:64, 0:512]), reads=[psk[7]], writes=["ns1"])
            I("pool", lambda e: e.dma_start(out=NSB[0][64:65, :], in_=tin["t_csh"][i][4 * g:4 * g + 1, :]), writes=["nsr0"], dma=True)
            kt, vb = KTB[0], VB[0]
            I("pool", lambda e, kt=kt: e.dma_start(out=kt[:, 0:Nk], in_=KT_SLC[g][:, 0:Nk]), writes=["ktb0h", "ktb0l"], dma=True)
            I("pool", lambda e, vb=vb: e.dma_start(out=vb[:, 0:nkb, :], in_=V_SLC[g][:, 0:nkb, :]), writes=["vb0h", "vb0l"], dma=True)
            I("pool", lambda e: e.dma_start(out=MSK[:, 0:8, :], in_=tin["t_slm"][i]), writes=["msk"], dma=True)
            for r in range(4):
                h = 4 * g + r
                nsb = NSB[r % 2]
                nsk = ["ns%d" % (r % 2), "nsr%d" % (r % 2)]
                if r < 3:
                    I("pool", lambda e: e.dma_start(out=NSB[(r + 1) % 2][64:65, :], in_=tin["t_csh"][i][h + 1:h + 2, :]),
                      writes=["nsr%d" % ((r + 1) % 2)], dma=True)

                def slA(kb):
                    diag = kb >= 8 * i
                    m = kb - 8 * i
                    bb = kb % 2
                    I("pe", lambda e: e.matmul(PSB[bb], lhsT=kt[:, kb * 128:(kb + 1) * 128], rhs=QT[:, 8 + h, :], start=True, stop=False),
                      reads=["ktb0h", "ktb0l", "qt%d" % (8 + h)], writes=[psk[bb]])
                    I("pe", lambda e: e.matmul(PSB[bb], lhsT=NEGEXP[0:65, kb * 128:(kb + 1) * 128], rhs=nsb[0:65, :], start=False, stop=(not diag)),
                      reads=["t_negexp"] + nsk, writes=[psk[bb]])
                    if diag:
                        I("pe", lambda e: e.matmul(PSB[bb], lhsT=IDENT[:, :], rhs=MSK[:, m, :], start=False, stop=True),
                          reads=["t_ident", "msk"], writes=[psk[bb]])

                def slB(kb):
                    bb = kb % 2
                    wb = WB[kb % 2]
                    rel = kb - 8 * i + OFFS
                    I("act", lambda e: e.activation(out=wb, in_=PSB[bb], func=AF.Exp, bias=BC[:, h * NREL + rel:h * NREL + rel + 1], scale=1.0),
                      reads=[psk[bb], "t_bc"], writes=["wb%d" % (kb % 2)])

                ob_, db_ = (2, 3) if r % 2 == 0 else (4, 5)

                def slC(kb):
                    wb = WB[kb % 2]
                    I("pe", lambda e: e.matmul(PSB[ob_], lhsT=vb[:, kb, :], rhs=wb, start=(kb == 0), stop=(kb == nkb - 1)),
                      reads=["vb0h", "vb0l", "wb%d" % (kb % 2)], writes=[psk[ob_]])
                    if kb == 0:
                        I("dve", lambda e: e.tensor_copy(out=E1[0], in_=wb), reads=["wb%d" % (kb % 2)], writes=["e10"])
                    else:
                        I("dve", lambda e: e.tensor_tensor(out=E1[0], in0=E1[0], in1=wb, op=ALU.add), reads=["e10", "wb%d" % (kb % 2)], writes=["e10"])

                slA(0)
                slB(0)
                for kb in range(nkb):
                    if kb + 1 < nkb:
                        slA(kb + 1)
                        slB(kb + 1)
                    slC(kb)
                I("pe", lambda e: e.matmul(PSB[db_], lhsT=ONESF[:, :], rhs=E1[0], start=True, stop=True), reads=["t_onesf", "e10"], writes=[psk[db_]])
                gate_rg(1, h, db_)
                acc_branch(r, ob_, False)
            kb_lo = max(8 * i - 4, 0)
            nwin = nkb - kb_lo
            kt, vb = KTB[1], VB[1]
            I("pool", lambda e, kt=kt: e.dma_start(out=kt[:, 0:nwin * 128], in_=KT_WIN[g][:, kb_lo * 128:nkb * 128]), writes=["ktb1"], dma=True)
            I("pool", lambda e, vb=vb: e.dma_start(out=vb[:, 0:nwin, :], in_=V_WIN[g][:, kb_lo:nkb, :]), writes=["vb1"], dma=True)
            I("pool", lambda e: e.dma_start(out=MSK[:, 0:12, :], in_=tin["t_wnm"][i]), writes=["msk"], dma=True)
            for r in range(4):
                h = 4 * g + r

                def wnA(n_):
                    kb = kb_lo + n_
                    m = kb - (8 * i - 4)
                    bb = n_ % 2
                    I("pe", lambda e: e.matmul(PSB[bb], lhsT=kt[:, n_ * 128:(n_ + 1) * 128], rhs=QT[:, 8 + h, :], start=True, stop=False),
                      reads=["ktb1", "qt%d" % (8 + h)], writes=[psk[bb]])
                    I("pe", lambda e: e.matmul(PSB[bb], lhsT=SELH[0:8, h * 128:(h + 1) * 128], rhs=CSH[0:8, :], start=False, stop=False),
                      reads=["t_selh", "csh"], writes=[psk[bb]])
                    I("pe", lambda e: e.matmul(PSB[bb], lhsT=IDENT[:, :], rhs=MSK[:, m, :], start=False, stop=True),
                      reads=["t_ident", "msk"], writes=[psk[bb]])

                def wnB(n_):
                    kb = kb_lo + n_
                    bb = n_ % 2
                    wb = WB[n_ % 2]
                    rel = kb - 8 * i + OFFS
                    I("act", lambda e: e.activation(out=wb, in_=PSB[bb], func=AF.Exp, bias=BC[:, h * NREL + rel:h * NREL + rel + 1], scale=1.0),
                      reads=[psk[bb], "t_bc"], writes=["wb%d" % (n_ % 2)])

                ob_, db_ = (2, 3) if r % 2 == 0 else (4, 5)

                def wnC(n_):
                    wb = WB[n_ % 2]
                    I("pe", lambda e: e.matmul(PSB[ob_], lhsT=vb[:, n_, :], rhs=wb, start=(n_ == 0), stop=(n_ == nwin - 1)),
                      reads=["vb1", "wb%d" % (n_ % 2)], writes=[psk[ob_]])
                    if n_ == 0:
                        I("dve", lambda e: e.tensor_copy(out=E1[0], in_=wb), reads=["wb%d" % (n_ % 2)], writes=["e10"])
                    else:
                        I("dve", lambda e: e.tensor_tensor(out=E1[0], in0=E1[0], in1=wb, op=ALU.add), reads=["e10", "wb%d" % (n_ % 2)], writes=["e10"])

                wnA(0)
                wnB(0)
                for n_ in range(nwin):
                    if n_ + 1 < nwin:
                        wnA(n_ + 1)
                        wnB(n_ + 1)
                    wnC(n_)
                I("pe", lambda e: e.matmul(PSB[db_], lhsT=ONESF[:, :], rhs=E1[0], start=True, stop=True), reads=["t_onesf", "e10"], writes=[psk[db_]])
                gate_rg(2, h, db_)
                acc_branch(r, ob_, False)
                I("act", lambda e, r=r, h=h: e.copy(out=OUTS[:, 8 + h, :], in_=ACC[r]), reads=["acc%d" % r], writes=["outs%d" % (8 + h)])
        if dbg:
            I("pool", lambda e: e.dma_start(out=dbg_out["d_sb"][i].rearrange("h p t -> p h t"), in_=OUTS[:, 0:8, :]),
              reads=["outs%d" % h for h in range(8)], writes=["dbg"], dma=True)
            I("pool", lambda e: e.dma_start(out=dbg_out["d_nsa"][i].rearrange("h p t -> p h t"), in_=OUTS[:, 8:16, :]),
              reads=["outs%d" % h for h in range(8, 16)], writes=["dbg"], dma=True)

        P.fence()
        rreset(PREFIX)
        hT = carve(16 * 512, BF16).rearrange("p (k t) -> p k t", k=16)
        GB = carve(D, F32)
        tmp = carve(D, F32)
        tmp2 = carve(D, F32)
        sg = [carve(512, F32), carve(512, F32)]
        FFN_A0 = roff[0]
        MG = carve(16 * 512, BF16).rearrange("p (k t) -> p k t", k=16)
        hbf4 = carve(4 * D, BF16).rearrange("p (a b) -> p a b", a=4)
        GT = [carve(512, F32) for _ in range(4)]
        load_GB(GB, 3)
        wv = wview(win)
        jobs = []
        for c in range(16):
            def load(c=c):
                h1 = stage2([wbsb.rearrange("(h p) c -> p h c", p=128)[:, :, c * 128:(c + 1) * 128],
                             wbnsa.rearrange("(h p) c -> p h c", p=128)[:, :, c * 128:(c + 1) * 128]],
                            [[128, 8, 128], [128, 8, 128]], ("wb", c))
                h2 = stage2([wv[:, :, C_MERGE + c * 128:C_MERGE + (c + 1) * 128],
                             wv[:, :, C_MERGE + 2048 + c * 128:C_MERGE + 2048 + (c + 1) * 128]],
                            [[128, 16, 128], [128, 16, 128]], ("wm", c))
                return h1, h2

            def comp(hh, c=c):
                ((wsb, ksb_), (wns, kns_)), ((wm0, km0), (wm1, km1)) = hh
                b0 = (c % 2) * 4
                for hd in range(8):
                    I("pe", lambda e, hd=hd, wsb=wsb, b0=b0: e.matmul(PSB[b0], lhsT=wsb[:, hd, :], rhs=OUTS[:, hd, :], start=(hd == 0), stop=(hd == 7)),
                      reads=ksb_ + ["outs%d" % hd], writes=[psk[b0]])
                for hd in range(8):
                    I("pe", lambda e, hd=hd, wns=wns, b0=b0: e.matmul(PSB[b0 + 1], lhsT=wns[:, hd, :], rhs=OUTS[:, 8 + hd, :], start=(hd == 0), stop=(hd == 7)),
                      reads=kns_ + ["outs%d" % (8 + hd)], writes=[psk[b0 + 1]])
                for jj, (wm, kmm) in enumerate(((wm0, km0), (wm1, km1))):
                    for k in range(16):
                        I("pe", lambda e, k=k, wm=wm, b=b0 + 2 + jj: e.matmul(PSB[b], lhsT=wm[:, k, :], rhs=H2[:, k, :], start=(k == 0), stop=(k == 15)),
                          reads=kmm + ["H2"], writes=[psk[b0 + 2 + jj]])
                for jj in range(2):
                    I("act", lambda e, jj=jj, b=b0 + 2 + jj: e.activation(out=GT[jj], in_=PSB[b], func=AF.Sigmoid), reads=[psk[b0 + 2 + jj]], writes=["gt%d" % jj])
                    I("dve", lambda e, jj=jj, b=b0 + jj: e.tensor_tensor(out=GT[2 + jj], in0=GT[jj], in1=PSB[b], op=ALU.mult),
                      reads=["gt%d" % jj, psk[b0 + jj]], writes=["gt%d" % (2 + jj)])
                I("dve", lambda e, c=c: e.tensor_tensor(out=MG[:, c, :], in0=GT[2], in1=GT[3], op=ALU.add), reads=["gt2", "gt3"], writes=["mg%d" % c])
            jobs.append((load, comp))
        pipeline(jobs, 1)

        def epi_post(ts, rs, k, tm, tk):
            xr = X1OWN[:, ts * D:(ts + 1) * D]
            resid_update(xr, ["x1own%d" % ts], rs, k, tm, tk)
            prenorm(xr, ["x1own%d" % ts], tm, tk, hbf4[:, ts, :], "hbf%d" % ts)

        tok_out(lambda k: MG[:, k, :], lambda k: ["mg%d" % k], 16, wout, epi_post, GB, [tmp, tmp2], 1.0)
        if dbg:
            for ts in range(4):
                I("pool", lambda e, ts=ts: e.dma_start(out=dbg_out["d_x2"][i * 512 + ts * 128:i * 512 + (ts + 1) * 128, :],
                                                     in_=X1OWN[:, ts * D:(ts + 1) * D]), reads=["x1own%d" % ts], writes=["dbg"], dma=True)
        for ts in range(4):
            transposes_to_hT(hbf4[:, ts, :], "hbf%d" % ts, hT, "hT", ts, 2)
        P.fence()
        rreset(FFN_A0)
        aT = carve(NFF * 512, BF16).rearrange("p (k t) -> p k t", k=NFF)
        ffn_in(w2i, hT, "hT", aT, sg)
        load_GB(GB, 5)

        def epi_f2(ts, rs, k, tm, tk):
            xr = X1OWN[:, ts * D:(ts + 1) * D]
            resid_update(xr, ["x1own%d" % ts], rs, k, tm, tk)
            I("pool", lambda e: e.dma_start(out=out[i * 512 + ts * 128:i * 512 + (ts + 1) * 128, :], in_=xr),
              reads=["x1own%d" % ts], dma=True)

        tok_out(lambda k: aT[:, k, :], lambda k: ["aT%d" % k], NFF, w2o, epi_f2, GB, [tmp, tmp2], 0.5)

    P.finish("sp")
    P.emit()
    return nc


_CACHE = {}


def _run(S, inputs, dbg=False):
    B = inputs["x"].shape[0]
    NSLOT = S // 1024
    key = (S, dbg)
    if key not in _CACHE:
        _CACHE[key] = build_program(S, dbg)
    nc = _CACHE[key]
    f = lambda a: np.ascontiguousarray(np.asarray(a, np.float32))
    gains = np.stack([f(inputs[k])[0] for k in ["ffn1_pre_g", "ffn1_post_g", "mix_pre_g", "mix_post_g", "ffn2_pre_g", "ffn2_post_g"]])
    shared = {"gains": gains}
    for k in ["ffn1_w_in", "ffn1_w_out", "ffn2_w_in", "ffn2_w_out", "w_in", "cmp_k_w1", "cmp_v_w1", "cmp_k_w2", "cmp_v_w2",
              "cmp_pos_k", "cmp_pos_v", "w_branch_sb", "w_branch_nsa", "w_out"]:
        shared[k] = f(inputs[k])[0]
    tabs = [make_tables(S, c) for c in range(2)]
    xin = f(inputs["x"])
    in_maps = []
    ncores = 2 * B
    for core in range(ncores):
        b, c = core // 2, core % 2
        m = dict(shared)
        m["x"] = xin[b]
        m.update(tabs[c])
        in_maps.append(m)
    res = run_bass_kernel_spmd(nc, in_maps, core_ids=list(range(ncores)))
    outp = np.zeros((B, S, D), np.float32)
    for core in range(ncores):
        b, c = core // 2, core % 2
        o = res.results[core]["out"]
        for i in range(NSLOT):
            t0 = 1024 * i + 512 * c
            outp[b, t0:t0 + 512] = o[i * 512:(i + 1) * 512]
    return outp, res


def kernel(**inputs):
    S = inputs["x"].shape[1]
    outp, _ = _run(S, inputs)
    return outp
```
